# Optimizing a Trainium2 kernel written in Bass

```python
import math
import jax, jax.numpy as jnp
from jax import lax
import numpy as np

D_MODEL = 1024
BATCH = 16
SEQ = 4096
DEPTH = 4

N_MEM = 256
GRID_W = 64
EPS = 1e-6
FOURIER_GROUPS = 4
FOURIER_GROUP_DIM = D_MODEL // 8
FOURIER_WIDTH = FOURIER_GROUPS * FOURIER_GROUP_DIM
NA_HEADS = 4
NA_HEAD_DIM = D_MODEL // 8
NA_WIDTH = NA_HEADS * NA_HEAD_DIM
NA_KH = 8
NA_KW = 16
AB_IN_WIDTH = FOURIER_WIDTH + 3 * NA_WIDTH
AB_OUT_WIDTH = FOURIER_WIDTH + NA_WIDTH
CONV_WIDTH = 3
XA_HEADS = 4
XA_HEAD_DIM = D_MODEL // XA_HEADS
D_FF = 2 * D_MODEL
N_EVEN = (DEPTH + 1) // 2
N_ODD = DEPTH // 2

kernel_name = "hybrid_fourier_natten_shortconv_encoder"


def rms_norm(x, g):
    xf = x.astype(jnp.float32)
    y = xf * lax.rsqrt(jnp.mean(xf * xf, axis=-1, keepdims=True) + EPS)
    return (y * g.astype(jnp.float32)).astype(x.dtype)


def dwconv3_centred(x, w):
    xp = jnp.pad(x, ((0, 0), (1, 1), (0, 0)))
    return xp[:, :-2] * w[0] + xp[:, 1:-1] * w[1] + xp[:, 2:] * w[2]


def fourier_mix(u):
    f = jnp.fft.fft2(u.astype(jnp.float32), axes=(1, 3), norm="ortho")
    return jnp.real(f).astype(u.dtype)


def neighborhood_attention(q, k, v, rpb):
    B, S, H, dh = q.shape
    rows = S // GRID_W
    kh = min(NA_KH, rows)
    kw = NA_KW
    qg = q.reshape(B, rows, GRID_W, H, dh)
    kg = k.reshape(B, rows, GRID_W, H, dh)
    vg = v.reshape(B, rows, GRID_W, H, dh)
    cols = jnp.arange(GRID_W)
    col_start = jnp.clip(cols - kw // 2, 0, GRID_W - kw)
    col_idx = col_start[:, None] + jnp.arange(kw)[None, :]
    col_off = col_idx - cols[:, None] + (NA_KW - 1)
    scale = dh ** -0.5

    def one_row(i):
        rs = jnp.clip(i - kh // 2, 0, rows - kh)
        q_i = lax.dynamic_index_in_dim(qg, i, axis=1, keepdims=False)
        k_blk = lax.dynamic_slice_in_dim(kg, rs, kh, axis=1)
        v_blk = lax.dynamic_slice_in_dim(vg, rs, kh, axis=1)
        k_win = k_blk[:, :, col_idx]
        v_win = v_blk[:, :, col_idx]
        s = jnp.einsum('bjhd,brjchd->bhjrc', q_i, k_win).astype(jnp.float32) * scale
        row_off = rs + jnp.arange(kh) - i + (NA_KH - 1)
        bias = rpb[:, row_off][:, :, col_off]
        s = s + jnp.transpose(bias, (0, 2, 1, 3)).astype(jnp.float32)[None]
        p = jax.nn.softmax(s.reshape(B, H, GRID_W, kh * kw), axis=-1)
        p = p.reshape(B, H, GRID_W, kh, kw).astype(v.dtype)
        return jnp.einsum('bhjrc,brjchd->bjhd', p, v_win)

    out = lax.map(one_row, jnp.arange(rows))
    return jnp.moveaxis(out, 0, 1).reshape(B, S, H, dh)


def fourier_na_mixer(h, w_in, rpb, w_out):
    B, S, _ = h.shape
    z = h @ w_in
    zf = z[..., :FOURIER_WIDTH]
    q = z[..., FOURIER_WIDTH:FOURIER_WIDTH + NA_WIDTH].reshape(B, S, NA_HEADS, NA_HEAD_DIM)
    k = z[..., FOURIER_WIDTH + NA_WIDTH:FOURIER_WIDTH + 2 * NA_WIDTH].reshape(B, S, NA_HEADS, NA_HEAD_DIM)
    v = z[..., FOURIER_WIDTH + 2 * NA_WIDTH:].reshape(B, S, NA_HEADS, NA_HEAD_DIM)
    yf = fourier_mix(zf.reshape(B, S, FOURIER_GROUPS, FOURIER_GROUP_DIM)).reshape(B, S, FOURIER_WIDTH)
    ya = neighborhood_attention(q, k, v, rpb).reshape(B, S, NA_WIDTH)
    return jnp.concatenate([yf, ya], axis=-1) @ w_out


def short_gated_conv_mixer(h, w_in, conv_w, w_out):
    z = h @ w_in
    gate_b = z[..., :D_MODEL]
    gate_c = z[..., D_MODEL:2 * D_MODEL]
    u = z[..., 2 * D_MODEL:]
    return (gate_b * dwconv3_centred(gate_c * u, conv_w)) @ w_out


def memory_cross_attention(h, m, wq, wkv, wo):
    B, S, D = h.shape
    q = (h @ wq).reshape(B, S, XA_HEADS, XA_HEAD_DIM)
    kv = m @ wkv
    k = kv[..., :D].reshape(B, -1, XA_HEADS, XA_HEAD_DIM)
    v = kv[..., D:].reshape(B, -1, XA_HEADS, XA_HEAD_DIM)
    s = jnp.einsum('bshd,bmhd->bhsm', q, k).astype(jnp.float32) * (XA_HEAD_DIM ** -0.5)
    p = jax.nn.softmax(s, axis=-1).astype(v.dtype)
    o = jnp.einsum('bhsm,bmhd->bshd', p, v).reshape(B, S, D)
    return o @ wo


def conv_ffn(h, w_up, conv_w, conv_b, w_down):
    z = h @ w_up
    u = z[..., :D_FF]
    g = dwconv3_centred(z[..., D_FF:], conv_w) + conv_b
    return (jax.nn.gelu(g, approximate=False) * u) @ w_down


def setup_inputs(seed: int = 0) -> dict:
    key = jax.random.key(seed)
    ks = jax.random.split(key, 24)
    f32 = jnp.float32
    nrm = lambda k, shape, fan_in: jax.random.normal(k, shape, f32) * (fan_in ** -0.5)
    gain = lambda k, shape: 1.0 + 0.1 * jax.random.normal(k, shape, f32)
    D = D_MODEL
    return {
        "x": jax.random.normal(ks[0], (BATCH, SEQ, D), f32),
        "mem": jax.random.normal(ks[1], (BATCH, N_MEM, D), f32),
        "mem_norm_g": gain(ks[2], (D,)),
        "mix_norm_g": gain(ks[3], (DEPTH, D)),
        "w_in_ab": nrm(ks[4], (N_EVEN, D, AB_IN_WIDTH), D),
        "rpb": 0.02 * jax.random.normal(ks[5], (N_EVEN, NA_HEADS, 2 * NA_KH - 1, 2 * NA_KW - 1), f32),
        "w_out_ab": nrm(ks[6], (N_EVEN, AB_OUT_WIDTH, D), AB_OUT_WIDTH),
        "w_in_c": nrm(ks[7], (N_ODD, D, 3 * D), D),
        "conv_c": nrm(ks[8], (N_ODD, CONV_WIDTH, D), CONV_WIDTH),
        "w_out_c": nrm(ks[9], (N_ODD, D, D), D),
        "xa_norm_g": gain(ks[10], (DEPTH, D)),
        "xa_wq": nrm(ks[11], (DEPTH, D, D), D),
        "xa_wkv": nrm(ks[12], (DEPTH, D, 2 * D), D),
        "xa_wo": nrm(ks[13], (DEPTH, D, D), D),
        "ffn_norm_g": gain(ks[14], (DEPTH, D)),
        "ffn_w_up": nrm(ks[15], (DEPTH, D, 2 * D_FF), D),
        "ffn_conv_w": nrm(ks[16], (DEPTH, CONV_WIDTH, D_FF), CONV_WIDTH),
        "ffn_conv_b": 0.02 * jax.random.normal(ks[17], (DEPTH, D_FF), f32),
        "ffn_w_down": nrm(ks[18], (DEPTH, D_FF, D), D_FF),
        "final_norm_g": gain(ks[19], (D,)),
    }


def reference(x, mem, mem_norm_g, mix_norm_g, w_in_ab, rpb, w_out_ab, w_in_c, conv_c, w_out_c,
              xa_norm_g, xa_wq, xa_wkv, xa_wo, ffn_norm_g, ffn_w_up, ffn_conv_w, ffn_conv_b,
              ffn_w_down, final_norm_g):
    m = rms_norm(mem, mem_norm_g)
    h = x
    for layer in range(DEPTH):
        hn = rms_norm(h, mix_norm_g[layer])
        if layer % 2 == 0:
            j = layer // 2
            h = h + fourier_na_mixer(hn, w_in_ab[j], rpb[j], w_out_ab[j])
        else:
            j = layer // 2
            h = h + short_gated_conv_mixer(hn, w_in_c[j], conv_c[j], w_out_c[j])
        h = h + memory_cross_attention(rms_norm(h, xa_norm_g[layer]), m,
                                       xa_wq[layer], xa_wkv[layer], xa_wo[layer])
        h = h + conv_ffn(rms_norm(h, ffn_norm_g[layer]), ffn_w_up[layer],
                         ffn_conv_w[layer], ffn_conv_b[layer], ffn_w_down[layer])
    return rms_norm(h, final_norm_g)
```

```python
import numpy as np
import ml_dtypes
from contextlib import ExitStack
import concourse.bass as bass
import concourse.mybir as mybir
from concourse.bass_utils import run_bass_kernel_spmd

F32 = mybir.dt.float32
BF16 = mybir.dt.bfloat16
AF = mybir.ActivationFunctionType
ALU = mybir.AluOpType

D = 1024
S = 4096
NSEQ = 2
NTOK = S * NSEQ
TT = 512
NT = S // TT
DEPTH = 4
NMEM = 256
DFF = 2048
EPS = 1e-6
NEG = -30000.0
NMI = 14


class Res:
    __slots__ = ("name", "writers", "readers", "sem", "cnt")

    def __init__(self, name=""):
        self.name = name
        self.writers = []
        self.readers = []
        self.sem = None
        self.cnt = 0


class Op:
    __slots__ = ("eng", "isdma", "fns", "deps", "sig", "sem", "val")


class Prog:
    NDSEM = 90

    def __init__(self, nc, es):
        self.nc = nc
        self.E = {"pe": nc.tensor, "act": nc.scalar, "dve": nc.vector, "pool": nc.gpsimd, "sp": nc.sync}
        self.ops = []
        self.esem = {e: es.enter_context(nc.semaphore("s_" + e)) for e in ["pe", "act", "dve", "pool"]}
        self.dsem = [es.enter_context(nc.semaphore("d%d" % i)) for i in range(self.NDSEM)]
        self.dcnt = [0] * self.NDSEM
        self.free = list(range(self.NDSEM))
        self.inuse = []
        self.last = {}
        self.dmas = []

    def op(self, eng, fns, reads=(), writes=()):
        if eng != "pe" and isinstance(fns, list) and len(fns) > 1:
            for f in fns:
                o = self.op(eng, f, reads, writes)
            return o
        o = Op()
        o.eng = eng
        o.isdma = False
        o.fns = fns if isinstance(fns, list) else [fns]
        o.deps = []
        o.sig = False
        o.sem = None
        o.val = 0
        for r in reads:
            o.deps += r.writers
            r.readers.append(o)
        for w in writes:
            o.deps += w.writers
            o.deps += w.readers
            w.writers = [o]
            w.readers = []
        self.ops.append(o)
        self.last[eng] = o
        return o

    def dma(self, q, out, in_, sb, dram=None, load=True, slow=False):
        o = Op()
        o.eng = q
        o.isdma = True
        o.deps = []
        o.sig = True
        eng = self.E[q]
        if slow:
            o.fns = [lambda: eng.dma_start(out=out, in_=in_, allow_slow_non_contiguous=True)]
        else:
            o.fns = [lambda: eng.dma_start(out=out, in_=in_)]
        if sb.sem is None:
            idx = self.free.pop()
            sb.sem = idx
            sb.cnt = self.dcnt[idx]
            self.inuse.append(sb)
        if load:
            if sb.readers or any(not w.isdma for w in sb.writers):
                o.deps += sb.readers + sb.writers
                sb.writers = [o]
                sb.readers = []
            else:
                sb.writers.append(o)
            if dram is not None:
                o.deps += dram.writers
                dram.readers.append(o)
        else:
            o.deps += sb.writers
            sb.readers.append(o)
            if dram is not None:
                o.deps += dram.writers + dram.readers
                dram.writers = [o]
                dram.readers = []
        sb.cnt += 16
        o.sem = self.dsem[sb.sem]
        o.val = sb.cnt
        self.ops.append(o)
        self.dmas.append(o)
        return o

    def barrier(self):
        lasts = list(self.last.values())
        dm = list(self.dmas)
        for e in ["pe", "act", "dve", "pool", "sp"]:
            o = Op()
            o.eng = e
            o.isdma = False
            o.fns = []
            o.deps = lasts + dm
            o.sig = False
            o.sem = None
            o.val = 0
            self.ops.append(o)
        self.dmas = []
        for r in self.inuse:
            self.dcnt[r.sem] = r.cnt
            self.free.append(r.sem)
            r.sem = None
        self.inuse = []

    def emit(self):
        for o in self.ops:
            for d in o.deps:
                if d is o or d.isdma:
                    continue
                if (not o.isdma) and d.eng == o.eng and o.eng == "pe":
                    continue
                d.sig = True
        cnt = {e: 0 for e in self.esem}
        for o in self.ops:
            if (not o.isdma) and o.sig:
                cnt[o.eng] += 1
                o.sem = self.esem[o.eng]
                o.val = cnt[o.eng]
        seen = {e: {} for e in self.E}
        nwait = 0
        for o in self.ops:
            eng = self.E[o.eng]
            need = {}
            for d in o.deps:
                if d is o:
                    continue
                if (not d.isdma) and (not o.isdma) and d.eng == o.eng and o.eng == "pe":
                    continue
                k = id(d.sem)
                if k not in need or need[k][1] < d.val:
                    need[k] = (d.sem, d.val)
            sn = seen[o.eng]
            for k, (sem, val) in need.items():
                if sn.get(k, 0) < val:
                    eng.wait_ge(sem, val)
                    sn[k] = val
                    nwait += 1
            n = len(o.fns)
            for i, f in enumerate(o.fns):
                ins = f()
                if i == n - 1 and o.sig:
                    ins.then_inc(o.sem, 16 if o.isdma else 1)
        return len(self.ops), nwait


def I(fn, *a, **kw):
    return lambda: fn(*a, **kw)


class Buf:
    CNT = [0]

    def __init__(self, es, nc, name, shape, dtype):
        Buf.CNT[0] += 1
        name = "%s_%d" % (name, Buf.CNT[0])
        self.t = es.enter_context(nc.sbuf_tensor(name, shape, dtype))
        self.r = Res(name)
        self.rk = None
        self.pw = None
        self.K = None

    def rc(self, c0, c1):
        p0, p1 = c0 // self.pw, (c1 - 1) // self.pw
        return [self.rk[(k, p)] for k in range(self.K) for p in range(p0, p1 + 1)]


def vec_layout():
    off = {}
    cur = [0]

    def add(name, n):
        off[name] = cur[0]
        cur[0] += n

    add("memg", 8)
    for l in range(DEPTH):
        add("mix%d" % l, 8)
        add("xa%d" % l, 8)
        add("ffn%d" % l, 8)
        for tap in range(3):
            add("fw%d_%d" % (l, tap), 16)
        add("fb%d" % l, 16)
    add("fin", 8)
    for j in range(2):
        for tap in range(3):
            add("cc%d_%d" % (j, tap), 8)
    return off, cur[0]


VOFF, NV = vec_layout()


def build(nphase=100, debug=False):
    nc = bass.Bass("TRN2", target_bir_lowering=False)

    def din(name, shape, dt=F32):
        return nc.dram_tensor(name, shape, dt, kind="ExternalInput").ap()

    xT = din("xT", [D, NTOK])
    memT = din("memT", [D, NSEQ * NMEM])
    vecs_d = din("vecs", [128, NV])
    w_in_ab = din("w_in_ab", [2, D, 2048])
    w_out_ab = din("w_out_ab", [2, D, D])
    w_in_c = din("w_in_c", [2, D, 3 * D])
    w_out_c = din("w_out_c", [2, D, D])
    xa_wq = din("xa_wq", [DEPTH, D, D])
    xa_wkv = din("xa_wkv", [DEPTH, D, 2 * D])
    xa_wo = din("xa_wo", [DEPTH, D, D])
    ffn_w_up = din("ffn_w_up", [DEPTH, D, 2 * DFF])
    ffn_w_down = din("ffn_w_down", [DEPTH, DFF, D])
    cosT = din("cosT", [S, S], BF16)
    nsinT = din("nsinT", [S, S], BF16)
    csc_d = din("csc", [128, 256])
    nab_d = din("nab", [2, 4, 2, 128, NMI * 64])
    outT = nc.dram_tensor("outT", [D, NTOK], F32, kind="ExternalOutput").ap()
    HA = nc.dram_tensor("HA", [D, NTOK], F32).ap()
    HB = nc.dram_tensor("HB", [D, NTOK], F32).ap()

    def fm(ap):
        return ap.rearrange("(k p) t -> p k t", p=128)

    xT3, HA3, HB3, outT3 = fm(xT), fm(HA), fm(HB), fm(outT)
    RX = [None] * 16
    RA = [Res("HA%d" % i) for i in range(16)]
    RB = [Res("HB%d" % i) for i in range(16)]

    es = ExitStack()
    with es:
        P = Prog(nc, es)
        V, G, A, T, PL = nc.vector, nc.gpsimd, nc.scalar, nc.tensor, nc.gpsimd
        banks = []
        for i in range(8):
            b = es.enter_context(nc.psum_tensor("bank%d" % i, [128, 512], F32))
            banks.append((b, Res("bank%d" % i)))
        vecs = Buf(es, nc, "vecs", [128, NV], F32)
        ones = Buf(es, nc, "ones", [128, 128], BF16)
        mT = [Buf(es, nc, "mT%d" % s, [128, 8, NMEM], BF16) for s in range(NSEQ)]
        P.dma("sp", vecs.t[:], vecs_d, sb=vecs.r)
        P.op("pool", I(G.memset, ones.t[:], 1.0), writes=[ones.r])
        epsb = Buf(es, nc, "epsb", [128, 1], F32)
        P.op("pool", I(G.memset, epsb.t[:], EPS), writes=[epsb.r])

        def vcol(name, k=0, n=1):
            o = VOFF[name] + k
            return vecs.t[:, o:o + n]

        cvt_rr = [0]

        def load_w(es2, dst, src, K, N, gain=None, stage=None, cols=None, pw=None, order=None):
            c0, c1 = cols if cols is not None else (0, N)
            src3 = src.rearrange("(k p) n -> p k n", p=128)
            dst.pw, dst.K = N, K
            dst.rk = {(k, 0): Res("w") for k in range(K)}
            for k in range(K):
                P.dma("pool", dst.t[:, k, :], src3[:, k, c0:c1], sb=dst.rk[(k, 0)])
            if gain is not None:
                for k in range(K):
                    e = ["dve", "act"][cvt_rr[0] % 2]
                    cvt_rr[0] += 1
                    g_ap = vcol(gain, k)
                    w_ap = dst.t[:, k, :]
                    if e == "act":
                        f = I(A.activation, out=w_ap, in_=w_ap, func=AF.Copy, scale=g_ap)
                    else:
                        f = I(V.tensor_scalar, out=w_ap, in0=w_ap, scalar1=g_ap, scalar2=None, op0=ALU.mult)
                    P.op(e, f, reads=[vecs.r], writes=[dst.rk[(k, 0)]])

        evac_rr = [0]

        def evac(out_ap, bank, reads=(), writes=()):
            bt, br = bank
            e = ["act", "dve"][evac_rr[0] % 2]
            evac_rr[0] += 1
            if len(out_ap.shape) == 2:
                src = bt[:, 0:out_ap.shape[-1]]
            else:
                src = bt[:].rearrange("p (g n) -> p g n", g=out_ap.shape[1])
            if e == "act":
                P.op("act", I(A.copy, out=out_ap, in_=src), reads=list(reads), writes=[br] + list(writes))
            else:
                P.op("dve", I(V.tensor_copy, out=out_ap, in_=src), reads=list(reads), writes=[br] + list(writes))

        def norm_a(hb, sq, n=TT):
            P.op("act", I(A.activation, out=sq.t[:, :, 0:n], in_=hb.t[:, :, 0:n], func=AF.Square), reads=[hb.r], writes=[sq.r])

        def norm_b(hb, sq, rstd, hn, ssbank, n=TT, col0=0):
            bt, br = ssbank
            P.op("pe", [I(T.matmul, bt[:, col0:col0 + n], ones.t[:], sq.t[:, k, 0:n], start=(k == 0), stop=(k == 7)) for k in range(8)],
                 reads=[sq.r, ones.r], writes=[br])
            P.op("act", I(A.activation, out=rstd.t[:, 0:n], in_=bt[:, col0:col0 + n], func=AF.Ln, scale=1.0 / D, bias=epsb.t[:, 0:1]),
                 reads=[epsb.r], writes=[br, rstd.r])
            P.op("act", I(A.activation, out=rstd.t[:, 0:n], in_=rstd.t[:, 0:n], func=AF.Exp, scale=-0.5), writes=[rstd.r])
            P.op("dve", I(V.tensor_tensor, out=hn.t[:, :, 0:n], in0=hb.t[:, :, 0:n],
                                                 in1=rstd.t[:, 0:n].unsqueeze(1).broadcast_to([128, 8, n]), op=ALU.mult),
                 reads=[hb.r, rstd.r], writes=[hn.r])

        def norm_tile(hb, sq, rstd, hn, ssbank, n=TT):
            norm_a(hb, sq, n)
            norm_b(hb, sq, rstd, hn, ssbank, n)

        def mm_group(bank, cols, lhs_list, rhs_list, reads, start=True, stop=True):
            bt, br = bank
            n = len(lhs_list)
            fns = [I(T.matmul, bt[:, cols[0]:cols[1]], lhs_list[i], rhs_list[i], start=(start and i == 0), stop=(stop and i == n - 1))
                   for i in range(n)]
            P.op("pe", fns, reads=reads, writes=[br])

        with ExitStack() as ph:
            mem3 = fm(memT)
            mb = Buf(ph, nc, "mb", [128, 8, NMEM], F32)
            msq = Buf(ph, nc, "msq", [128, 8, NMEM], BF16)
            mrs = Buf(ph, nc, "mrs", [128, NMEM], F32)
            for s in range(NSEQ):
                P.dma("sp", mb.t[:], mem3[:, :, s * NMEM:(s + 1) * NMEM], sb=mb.r)
                norm_tile(mb, msq, mrs, mT[s], banks[0], n=NMEM)
            P.barrier()

        def final_norm(Hf3, Rf, out3, tiles):
            with ExitStack() as ph:
                hb = [Buf(ph, nc, "hb%d" % i, [128, 8, TT], F32) for i in range(2)]
                ob = [Buf(ph, nc, "ob%d" % i, [128, 8, TT], F32) for i in range(2)]
                sq = Buf(ph, nc, "sq", [128, 8, TT], BF16)
                rstd = Buf(ph, nc, "rstd", [128, TT], F32)
                for n_, i in enumerate(tiles):
                    b = n_ % 2
                    t0 = i * TT
                    o0 = (n_ if out3 is not outT3 else i) * TT
                    P.dma("sp", hb[b].t[:], Hf3[:, :, t0:t0 + TT], sb=hb[b].r, dram=Rf[i])
                    bt, br = banks[n_ % 2]
                    P.op("act", I(A.activation, out=sq.t[:], in_=hb[b].t[:], func=AF.Square), reads=[hb[b].r], writes=[sq.r])
                    P.op("pe", [I(T.matmul, bt[:], ones.t[:], sq.t[:, k, :], start=(k == 0), stop=(k == 7)) for k in range(8)],
                         reads=[sq.r, ones.r], writes=[br])
                    P.op("act", I(A.activation, out=rstd.t[:], in_=bt[:], func=AF.Sqrt, scale=1.0 / D, bias=epsb.t[:, 0:1]),
                         reads=[epsb.r], writes=[br, rstd.r])
                    P.op("dve", I(V.reciprocal, out=rstd.t[:], in_=rstd.t[:]), writes=[rstd.r])
                    P.op("dve", [I(V.scalar_tensor_tensor, out=ob[b].t[:, k, :], in0=hb[b].t[:, k, :], scalar=vcol("fin", k), in1=rstd.t[:], op0=ALU.mult, op1=ALU.mult)
                                 for k in range(8)], reads=[hb[b].r, rstd.r, vecs.r], writes=[ob[b].r])
                    P.dma("pool", out3[:, :, o0:o0 + TT], ob[b].t[:], sb=ob[b].r, load=False)
                P.barrier()

        def dbg_dump():
            if debug:
                k = phase[0]
                d3 = fm(nc.dram_tensor("dbg%d" % k, [D, NTOK], F32, kind="ExternalOutput").ap())
                Hf3, Rf = last_h
                with ExitStack() as ph:
                    hb = [Buf(ph, nc, "hb%d" % i, [128, 8, TT], F32) for i in range(2)]
                    for i in range(2 * NT):
                        b = i % 2
                        P.dma("sp", hb[b].t[:], Hf3[:, :, i * TT:(i + 1) * TT], sb=hb[b].r, dram=Rf[i])
                        P.dma("pool", d3[:, :, i * TT:(i + 1) * TT], hb[b].t[:], sb=hb[b].r, load=False)
                    P.barrier()

        phase = [0]

        def more():
            phase[0] += 1
            return phase[0] <= nphase

        last_h = [xT3, RX]

        for layer in range(DEPTH):
            j = layer // 2
            Hin3, Rin = (xT3, RX) if layer == 0 else (HA3, RA)
            if not more():
                break
            if layer % 2 == 0:
                for s in range(NSEQ):
                    tb = s * S
                    with ExitStack() as ph:
                        Pb = Buf(ph, nc, "Pb", [128, 32, 4, 256], BF16)
                        stage = None
                        with ExitStack() as p1:
                            Wz = Buf(p1, nc, "Wz", [128, 8, 512], BF16)
                            csc = Buf(p1, nc, "cscb", [128, 256], BF16)
                            cscf = Buf(p1, nc, "cscf", [128, 256], F32)
                            hb = [Buf(p1, nc, "hb%d" % i, [128, 8, TT], F32) for i in range(2)]
                            sq = Buf(p1, nc, "sq", [128, 8, TT], BF16)
                            rstd = Buf(p1, nc, "rstd", [128, TT], F32)
                            hn = [Buf(p1, nc, "hn%d" % i, [128, 8, TT], BF16) for i in range(2)]
                            zf = [Buf(p1, nc, "zf%d" % i, [128, 4, TT], BF16) for i in range(2)]
                            P.dma("sp", cscf.t[:], csc_d, sb=cscf.r)
                            P.op("dve", I(V.tensor_copy, out=csc.t[:], in_=cscf.t[:]), reads=[cscf.r], writes=[csc.r])
                            load_w(p1, Wz, w_in_ab[j], 8, 512, gain="mix%d" % layer, stage=stage, cols=(0, 512))
                            P.dma("sp", hb[0].t[:], Hin3[:, :, tb:tb + TT], sb=hb[0].r, dram=Rin[s * NT])
                            for i in range(NT):
                                b = i % 2
                                if i + 1 < NT:
                                    t1 = tb + (i + 1) * TT
                                    P.dma("sp", hb[1 - b].t[:], Hin3[:, :, t1:t1 + TT], sb=hb[1 - b].r, dram=Rin[s * NT + i + 1])
                                if i == 0:
                                    norm_tile(hb[0], sq, rstd, hn[0], banks[0])
                                if i + 1 < NT:
                                    norm_a(hb[1 - b], sq)
                                for g in range(4):
                                    bk = banks[1 + g % 2]
                                    mm_group(bk, (0, TT), [Wz.t[:, k, g * 128:(g + 1) * 128] for k in range(8)],
                                             [hn[b].t[:, k, :] for k in range(8)], reads=Wz.rc(g * 128, (g + 1) * 128) + [hn[b].r])
                                    evac(zf[b].t[:, g, :], bk, writes=[zf[b].r])
                                if i + 1 < NT:
                                    norm_b(hb[1 - b], sq, rstd, hn[1 - b], banks[0])
                                for c4 in range(4):
                                    for gp in range(2):
                                        bk = banks[3 + (c4 * 2 + gp) % 4]
                                        for gi in range(2):
                                            g = gp * 2 + gi
                                            mm_group(bk, (gi * 256, gi * 256 + 256), [zf[b].t[:, g, c4 * 128:(c4 + 1) * 128]], [csc.t[:]],
                                                     reads=[zf[b].r, csc.r])
                                        evac(Pb.t[:, i * 4 + c4, gp * 2:gp * 2 + 2, :], bk, writes=[Pb.r])
                            P.barrier()
                        with ExitStack() as p2:
                            Wo = Buf(p2, nc, "Wo", [128, 4, D], BF16)
                            tab = [Buf(p2, nc, "tab%d" % i, [128, 2, 8, TT], BF16) for i in range(2)]
                            YT = [Buf(p2, nc, "YT%d" % i, [128, 4, TT], BF16) for i in range(2)]
                            hb = [Buf(p2, nc, "hc%d" % i, [128, 8, TT], F32) for i in range(2)]
                            load_w(p2, Wo, w_out_ab[j][0:512, :], 4, D, stage=stage)
                            cos3 = cosT.rearrange("(a p) t -> p a t", p=128)
                            sin3 = nsinT.rearrange("(a p) t -> p a t", p=128)
                            tcnt = 0
                            for i in range(NT):
                                b = i % 2
                                t0 = tb + i * TT
                                P.dma("sp", hb[b].t[:], Hin3[:, :, t0:t0 + TT], sb=hb[b].r, dram=Rin[s * NT + i])
                                for q in range(4):
                                    tbf = tab[tcnt % 2]
                                    tcnt += 1
                                    P.dma("sp", tbf.t[:, 0, :, :], cos3[:, q * 8:(q + 1) * 8, i * TT:(i + 1) * TT], sb=tbf.r)
                                    P.dma("sp", tbf.t[:, 1, :, :], sin3[:, q * 8:(q + 1) * 8, i * TT:(i + 1) * TT], sb=tbf.r)
                                    for g in range(4):
                                        lhs, rhs = [], []
                                        for a in range(8):
                                            lhs.append(Pb.t[:, q * 8 + a, g, 0:128])
                                            rhs.append(tbf.t[:, 0, a, :])
                                            lhs.append(Pb.t[:, q * 8 + a, g, 128:256])
                                            rhs.append(tbf.t[:, 1, a, :])
                                        mm_group(banks[g], (0, TT), lhs, rhs, reads=[Pb.r, tbf.r], start=(q == 0), stop=(q == 3))
                                for g in range(4):
                                    evac(YT[b].t[:, g, :], banks[g], writes=[YT[b].r])
                                for oc in range(8):
                                    bk = banks[4 + oc % 4]
                                    mm_group(bk, (0, TT), [Wo.t[:, g, oc * 128:(oc + 1) * 128] for g in range(4)],
                                             [YT[b].t[:, g, :] for g in range(4)], reads=Wo.rc(oc * 128, (oc + 1) * 128) + [YT[b].r])
                                    P.op("dve", I(V.tensor_tensor, out=hb[b].t[:, oc, :], in0=bk[0][:], in1=hb[b].t[:, oc, :], op=ALU.add),
                                         writes=[bk[1], hb[b].r])
                                P.dma("pool", HB3[:, :, t0:t0 + TT], hb[b].t[:], sb=hb[b].r, dram=RB[s * NT + i], load=False)
                            P.barrier()
                    with ExitStack() as ph:
                        qT = Buf(ph, nc, "qT", [128, 4, S], BF16)
                        kT = Buf(ph, nc, "kT", [128, 4, S], BF16)
                        vv = Buf(ph, nc, "vv", [128, 32, 512], BF16)
                        stage = None
                        with ExitStack() as p1:
                            Wq = Buf(p1, nc, "Wqkv", [128, 8, 1536], BF16)
                            hb = [Buf(p1, nc, "hb%d" % i, [128, 8, TT], F32) for i in range(2)]
                            sq = Buf(p1, nc, "sq", [128, 8, TT], BF16)
                            rstd = Buf(p1, nc, "rstd", [128, TT], F32)
                            hn = [Buf(p1, nc, "hn%d" % i, [128, 8, TT], BF16) for i in range(2)]
                            load_w(p1, Wq, w_in_ab[j], 8, 1536, gain="mix%d" % layer, stage=stage, cols=(512, 2048))
                            P.dma("sp", hb[0].t[:], Hin3[:, :, tb:tb + TT], sb=hb[0].r, dram=Rin[s * NT])
                            for i in range(NT):
                                b = i % 2
                                if i + 1 < NT:
                                    t1 = tb + (i + 1) * TT
                                    P.dma("sp", hb[1 - b].t[:], Hin3[:, :, t1:t1 + TT], sb=hb[1 - b].r, dram=Rin[s * NT + i + 1])
                                if i == 0:
                                    norm_tile(hb[0], sq, rstd, hn[0], banks[0])
                                if i + 1 < NT:
                                    norm_a(hb[1 - b], sq)
                                nb = 0
                                for dst, c0 in ((qT, 0), (kT, 512)):
                                    for h in range(4):
                                        bk = banks[1 + nb % 7]
                                        nb += 1
                                        mm_group(bk, (0, TT), [Wq.t[:, k, c0 + h * 128:c0 + (h + 1) * 128] for k in range(8)],
                                                 [hn[b].t[:, k, :] for k in range(8)], reads=Wq.rc(c0 + h * 128, c0 + (h + 1) * 128) + [hn[b].r])
                                        evac(dst.t[:, h, i * TT:(i + 1) * TT], bk, writes=[dst.r])
                                if i + 1 < NT:
                                    norm_b(hb[1 - b], sq, rstd, hn[1 - b], banks[0])
                                for c4 in range(4):
                                    bk = banks[1 + nb % 7]
                                    nb += 1
                                    mm_group(bk, (0, 512), [hn[b].t[:, k, c4 * 128:(c4 + 1) * 128] for k in range(8)],
                                             [Wq.t[:, k, 1024:1536] for k in range(8)], reads=Wq.rc(1024, 1536) + [hn[b].r])
                                    evac(vv.t[:, i * 4 + c4, :], bk, writes=[vv.r])
                            P.barrier()
                        with ExitStack() as p2:
                            Ei = Buf(p2, nc, "Ei", [128, 4, 2, NMI * 64], BF16)
                            nst = [Buf(p2, nc, "nst%d" % i, [128, NMI * 64], F32) for i in range(2)]
                            Wo = Buf(p2, nc, "Wo2", [128, 4, D], BF16)
                            Pt = [Buf(p2, nc, "Pt%d" % i, [128, 6, 256], BF16) for i in range(3)]
                            ya = [Buf(p2, nc, "ya%d" % i, [128, 4, TT], BF16) for i in range(2)]
                            rden = [Buf(p2, nc, "rden%d" % i, [128, 256], F32) for i in range(3)]
                            hb = [Buf(p2, nc, "hc%d" % i, [128, 8, TT], F32) for i in range(2)]
                            load_w(p2, Wo, w_out_ab[j][512:1024, :], 4, D, stage=stage)
                            n_ = 0
                            for h in range(4):
                                for ty in range(2):
                                    st = nst[n_ % 2]
                                    n_ += 1
                                    P.dma("sp", st.t[:], nab_d[j, h, ty], sb=st.r)
                                    P.op("act", I(A.activation, out=Ei.t[:, h, ty, :], in_=st.t[:], func=AF.Exp),
                                         reads=[st.r], writes=[Ei.r])
                            scale = 128.0 ** -0.5
                            blocks = [(i, h, bl) for i in range(NT) for h in range(4) for bl in range(2)]

                            def chunks_of(blk):
                                if blk == 0:
                                    return [(c, c, 1, 6 - 2 * c) for c in range(4)]
                                if blk == 15:
                                    return [(c, 28 + c, 1, 10 - 2 * c) for c in range(4)]
                                return [(c, 2 * blk - 2 + c, 0, 10 - 2 * c) for c in range(6)]

                            def issue_S(n):
                                i, h, bl = blocks[n]
                                blk = 2 * i + bl
                                ch = chunks_of(blk)
                                set_ = n % 2
                                pt = Pt[n % 3]
                                for pr in range(len(ch) // 2):
                                    bk = banks[set_ * 3 + pr]
                                    for ci in range(2):
                                        c, kci, ty, mi0 = ch[pr * 2 + ci]
                                        mm_group(bk, (ci * 256, ci * 256 + 256), [kT.t[:, h, kci * 128:(kci + 1) * 128]],
                                                 [qT.t[:, h, blk * 256:(blk + 1) * 256]], reads=[kT.r, qT.r])
                                    P.op("act", I(A.activation,
                                        out=pt.t[:, 2 * pr:2 * pr + 2, :], in_=bk[0][:].rearrange("p (c n) -> p c n", c=2), func=AF.Exp, scale=scale),
                                        writes=[bk[1], pt.r])
                                    for ci in range(2):
                                        c, kci, ty, mi0 = ch[pr * 2 + ci]
                                        e = "dve"
                                        eng = V if e == "dve" else G
                                        P.op(e, I(eng.tensor_tensor,
                                            out=pt.t[:, c, :], in0=pt.t[:, c, :], in1=Ei.t[:, h, ty, mi0 * 64:mi0 * 64 + 256], op=ALU.mult),
                                            reads=[Ei.r], writes=[pt.r])

                            def issue_DO(n):
                                i, h, bl = blocks[n]
                                blk = 2 * i + bl
                                ch = chunks_of(blk)
                                pt = Pt[n % 3]
                                bk = banks[6 + n % 2]
                                mm_group(bk, (0, 256), [ones.t[:] for _ in ch], [pt.t[:, c, :] for (c, _, _, _) in ch], reads=[pt.r, ones.r])
                                mm_group(bk, (256, 512), [vv.t[:, kci, h * 128:(h + 1) * 128] for (_, kci, _, _) in ch],
                                         [pt.t[:, c, :] for (c, _, _, _) in ch], reads=[pt.r, vv.r])
                                rd = rden[n % 3]
                                yb = ya[i % 2]
                                P.op("act", I(A.activation, out=rd.t[:], in_=bk[0][:, 0:256], func=AF.Ln), writes=[bk[1], rd.r])
                                P.op("act", I(A.activation, out=rd.t[:], in_=rd.t[:], func=AF.Exp, scale=-1.0), writes=[rd.r])
                                P.op("dve", I(V.tensor_tensor, out=yb.t[:, h, bl * 256:(bl + 1) * 256], in0=bk[0][:, 256:512], in1=rd.t[:], op=ALU.mult),
                                     reads=[rd.r], writes=[bk[1], yb.r])

                            issue_S(0)
                            issue_S(1)
                            for n in range(len(blocks)):
                                i, h, bl = blocks[n]
                                b = i % 2
                                t0 = tb + i * TT
                                if h == 0 and bl == 0:
                                    P.dma("sp", hb[b].t[:], HB3[:, :, t0:t0 + TT], sb=hb[b].r, dram=RB[s * NT + i])
                                if n + 2 < len(blocks):
                                    issue_S(n + 2)
                                issue_DO(n)
                                if h == 3 and bl == 1:
                                    for oc in range(8):
                                        bk = banks[7]
                                        mm_group(bk, (0, TT), [Wo.t[:, hh, oc * 128:(oc + 1) * 128] for hh in range(4)],
                                                 [ya[b].t[:, hh, :] for hh in range(4)], reads=Wo.rc(oc * 128, (oc + 1) * 128) + [ya[b].r])
                                        P.op("dve", I(V.tensor_tensor, out=hb[b].t[:, oc, :], in0=bk[0][:], in1=hb[b].t[:, oc, :], op=ALU.add),
                                             writes=[bk[1], hb[b].r])
                                    P.dma("pool", HB3[:, :, t0:t0 + TT], hb[b].t[:], sb=hb[b].r, dram=RB[s * NT + i], load=False)
                            P.barrier()
            else:
                with ExitStack() as ph:
                    stage = None
                    Wi = Buf(ph, nc, "Wi", [128, 8, 3 * D], BF16)
                    Wo = Buf(ph, nc, "Woc", [128, 8, D], BF16)
                    hb = [Buf(ph, nc, "hb%d" % i, [128, 8, TT], F32) for i in range(2)]
                    hh = [Buf(ph, nc, "hh%d" % i, [128, 8, 2], F32) for i in range(2)]
                    sq = Buf(ph, nc, "sq", [128, 8, TT], BF16)
                    sqh = Buf(ph, nc, "sqh", [128, 8, 2], BF16)
                    rstd = Buf(ph, nc, "rstd", [128, TT], F32)
                    rsth = Buf(ph, nc, "rsth", [128, 2], F32)
                    hn = [Buf(ph, nc, "hn%d" % i, [128, 8, TT], BF16) for i in range(2)]
                    hnh = [Buf(ph, nc, "hnh%d" % i, [128, 8, 2], BF16) for i in range(2)]
                    Wc = [Buf(ph, nc, "Wc%d" % i, [128, TT + 2], F32) for i in range(2)]
                    Tc = [Buf(ph, nc, "Tc%d" % i, [128, TT], F32) for i in range(2)]
                    Uh = [Buf(ph, nc, "Uh%d" % i, [128, 2], F32) for i in range(2)]
                    Bs = [Buf(ph, nc, "Bs%d" % i, [128, TT], BF16) for i in range(2)]
                    yT = [Buf(ph, nc, "yT%d" % i, [128, 8, TT], BF16) for i in range(2)]
                    load_w(ph, Wi, w_in_c[j], 8, 3 * D, gain="mix%d" % layer, stage=stage, order=[0, 2, 4, 1, 3, 5])
                    load_w(ph, Wo, w_out_c[j], 8, D, stage=stage)

                    def load_tile(i, b):
                        s, ii = divmod(i, NT)
                        t0 = i * TT
                        P.dma("sp", hb[b].t[:], Hin3[:, :, t0:t0 + TT], sb=hb[b].r, dram=Rin[i])
                        P.op("pool", I(G.memset, hh[b].t[:], 0.0), writes=[hh[b].r])
                        if ii > 0:
                            P.dma("sp", hh[b].t[:, :, 0:1], Hin3[:, :, t0 - 1:t0], sb=hh[b].r, dram=Rin[i - 1], slow=True)
                        if ii < NT - 1:
                            P.dma("sp", hh[b].t[:, :, 1:2], Hin3[:, :, t0 + TT:t0 + TT + 1], sb=hh[b].r, dram=Rin[i + 1], slow=True)

                    load_tile(0, 0)
                    for i in range(2 * NT):
                        b = i % 2
                        t0 = i * TT
                        if i + 1 < 2 * NT:
                            load_tile(i + 1, 1 - b)
                        if i == 0:
                            norm_tile(hb[0], sq, rstd, hn[0], banks[0])
                            norm_tile(hh[0], sqh, rsth, hnh[0], banks[1], n=2)
                        for c in range(8):
                            if c == 5 and i + 1 < 2 * NT:
                                norm_a(hb[1 - b], sq)
                                norm_a(hh[1 - b], sqh, n=2)
                            bc, bu, bb_ = banks[2 + c % 2], banks[4 + c % 2], banks[6 + c % 2]
                            wc, tc = Wc[c % 2], Tc[c % 2]
                            mm_group(bc, (0, TT), [Wi.t[:, k, D + c * 128:D + (c + 1) * 128] for k in range(8)], [hn[b].t[:, k, :] for k in range(8)], reads=Wi.rc(D + c * 128, D + (c + 1) * 128) + [hn[b].r])
                            mm_group(banks[1], (8, 10), [Wi.t[:, k, D + c * 128:D + (c + 1) * 128] for k in range(8)], [hnh[b].t[:, k, :] for k in range(8)], reads=Wi.rc(D + c * 128, D + (c + 1) * 128) + [hnh[b].r])
                            mm_group(banks[1], (16, 18), [Wi.t[:, k, 2 * D + c * 128:2 * D + (c + 1) * 128] for k in range(8)], [hnh[b].t[:, k, :] for k in range(8)], reads=Wi.rc(2 * D + c * 128, 2 * D + (c + 1) * 128) + [hnh[b].r])
                            mm_group(bu, (0, TT), [Wi.t[:, k, 2 * D + c * 128:2 * D + (c + 1) * 128] for k in range(8)], [hn[b].t[:, k, :] for k in range(8)], reads=Wi.rc(2 * D + c * 128, 2 * D + (c + 1) * 128) + [hn[b].r])
                            mm_group(bb_, (0, TT), [Wi.t[:, k, c * 128:(c + 1) * 128] for k in range(8)], [hn[b].t[:, k, :] for k in range(8)], reads=Wi.rc(c * 128, (c + 1) * 128) + [hn[b].r])
                            P.op("act", I(A.copy, out=wc.t[:, 1:TT + 1], in_=bc[0][:]), writes=[bc[1], wc.r])
                            bs = Bs[c % 2]
                            P.op("act", I(A.copy, out=bs.t[:], in_=bb_[0][:]), writes=[bb_[1], bs.r])
                            uh = Uh[c % 2]
                            P.op("act", I(A.copy, out=uh.t[:], in_=banks[1][0][:, 16:18]), writes=[banks[1][1], uh.r])
                            P.op("dve", I(V.tensor_tensor, out=wc.t[:, 0:TT + 2:TT + 1], in0=banks[1][0][:, 8:10], in1=uh.t[:], op=ALU.mult),
                                 reads=[uh.r], writes=[banks[1][1], wc.r])
                            P.op("dve", I(V.tensor_tensor, out=wc.t[:, 1:TT + 1], in0=bu[0][:], in1=wc.t[:, 1:TT + 1], op=ALU.mult),
                                 writes=[bu[1], wc.r])
                            P.op("act", I(A.activation, out=tc.t[:], in_=wc.t[:, 1:TT + 1], func=AF.Copy, scale=vcol("cc%d_1" % j, c)),
                                 reads=[wc.r, vecs.r], writes=[tc.r])
                            P.op("dve", [I(V.scalar_tensor_tensor, out=tc.t[:], in0=wc.t[:, 0:TT], scalar=vcol("cc%d_0" % j, c), in1=tc.t[:], op0=ALU.mult, op1=ALU.add),
                                         I(V.scalar_tensor_tensor, out=tc.t[:], in0=wc.t[:, 2:TT + 2], scalar=vcol("cc%d_2" % j, c), in1=tc.t[:], op0=ALU.mult, op1=ALU.add)],
                                 reads=[wc.r, vecs.r], writes=[tc.r])
                            P.op("dve", I(V.tensor_tensor, out=yT[b].t[:, c, :], in0=bs.t[:], in1=tc.t[:], op=ALU.mult),
                                 reads=[tc.r, bs.r], writes=[yT[b].r])
                        if i + 1 < 2 * NT:
                            norm_b(hb[1 - b], sq, rstd, hn[1 - b], banks[0])
                            norm_b(hh[1 - b], sqh, rsth, hnh[1 - b], banks[0], n=2, col0=0)
                        for oc in range(8):
                            bk = banks[2 + oc % 6]
                            mm_group(bk, (0, TT), [Wo.t[:, k, oc * 128:(oc + 1) * 128] for k in range(8)], [yT[b].t[:, k, :] for k in range(8)], reads=Wo.rc(oc * 128, (oc + 1) * 128) + [yT[b].r])
                            P.op("dve", I(V.tensor_tensor, out=hb[b].t[:, oc, :], in0=bk[0][:], in1=hb[b].t[:, oc, :], op=ALU.add),
                                 writes=[bk[1], hb[b].r])
                        P.dma("pool", HB3[:, :, t0:t0 + TT], hb[b].t[:], sb=hb[b].r, dram=RB[i], load=False)
                    P.barrier()
            last_h = [HB3, RB]
            dbg_dump()
            if not more():
                break
            with ExitStack() as ph:
                stage = None
                Wq = Buf(ph, nc, "xWq", [128, 8, D], BF16)
                Wo = Buf(ph, nc, "xWo", [128, 8, D], BF16)
                kTm = [Buf(ph, nc, "kTm%d" % s, [128, 8, NMEM], BF16) for s in range(NSEQ)]
                vm = [Buf(ph, nc, "vm%d" % s, [128, 2, D], BF16) for s in range(NSEQ)]
                hb = [Buf(ph, nc, "hb%d" % i, [128, 8, TT], F32) for i in range(3)]
                sq = Buf(ph, nc, "sq", [128, 8, TT], BF16)
                rstd = Buf(ph, nc, "rstd", [128, TT], F32)
                hn = [Buf(ph, nc, "hn%d" % i, [128, 8, TT], BF16) for i in range(2)]
                qx = Buf(ph, nc, "qx", [128, 8, TT], BF16)
                px = [Buf(ph, nc, "px%d" % i, [128, 2, TT], BF16) for i in range(2)]
                ao = [Buf(ph, nc, "ao%d" % i, [128, 8, TT], BF16) for i in range(2)]
                rden = [Buf(ph, nc, "rdx%d" % i, [128, TT], F32) for i in range(2)]
                with ExitStack() as p1:
                    Wkv = Buf(p1, nc, "Wkv", [128, 8, 2 * D], BF16)
                    load_w(p1, Wkv, xa_wkv[layer], 8, 2 * D, gain="memg", stage=stage)
                    load_w(ph, Wq, xa_wq[layer], 8, D, gain="xa%d" % layer, stage=stage)
                    load_w(ph, Wo, xa_wo[layer], 8, D, stage=stage)
                    P.dma("sp", hb[0].t[:], HB3[:, :, 0:TT], sb=hb[0].r, dram=RB[0])
                    nb = 0
                    for s in range(NSEQ):
                        for oc in range(8):
                            bk = banks[nb % 8]
                            nb += 1
                            mm_group(bk, (0, NMEM), [Wkv.t[:, k, oc * 128:(oc + 1) * 128] for k in range(8)], [mT[s].t[:, k, :] for k in range(8)], reads=Wkv.rc(oc * 128, (oc + 1) * 128) + [mT[s].r])
                            evac(kTm[s].t[:, oc, :], bk, writes=[kTm[s].r])
                        for mc in range(2):
                            for hf in range(2):
                                bk = banks[nb % 8]
                                nb += 1
                                mm_group(bk, (0, 512), [mT[s].t[:, k, mc * 128:(mc + 1) * 128] for k in range(8)],
                                         [Wkv.t[:, k, D + hf * 512:D + (hf + 1) * 512] for k in range(8)], reads=Wkv.rc(D + hf * 512, D + (hf + 1) * 512) + [mT[s].r])
                                evac(vm[s].t[:, mc, hf * 512:(hf + 1) * 512], bk, writes=[vm[s].r])
                xscale = 256.0 ** -0.5
                NTL = 2 * NT

                def wo_part(i, ocs):
                    hbi = hb[i % 3]
                    aoi = ao[i % 2]
                    for oc in ocs:
                        bk = banks[1 + oc % 2]
                        mm_group(bk, (0, TT), [Wo.t[:, k, oc * 128:(oc + 1) * 128] for k in range(8)], [aoi.t[:, k, :] for k in range(8)],
                                 reads=Wo.rc(oc * 128, (oc + 1) * 128) + [aoi.r])
                        P.op("dve", I(V.tensor_tensor, out=hbi.t[:, oc, :], in0=bk[0][:], in1=hbi.t[:, oc, :], op=ALU.add),
                             writes=[bk[1], hbi.r])
                    if ocs[-1] == 7:
                        P.dma("pool", HB3[:, :, i * TT:(i + 1) * TT], hbi.t[:], sb=hbi.r, dram=RB[i], load=False)

                norm_tile(hb[0], sq, rstd, hn[0], banks[0])
                for i in range(NTL):
                    b = i % 2
                    s = i // NT
                    if i + 1 < NTL:
                        P.dma("sp", hb[(i + 1) % 3].t[:], HB3[:, :, (i + 1) * TT:(i + 2) * TT], sb=hb[(i + 1) % 3].r, dram=RB[i + 1])
                    for oc in range(8):
                        bk = banks[1 + oc % 2]
                        mm_group(bk, (0, TT), [Wq.t[:, k, oc * 128:(oc + 1) * 128] for k in range(8)], [hn[b].t[:, k, :] for k in range(8)],
                                 reads=Wq.rc(oc * 128, (oc + 1) * 128) + [hn[b].r])
                        evac(qx.t[:, oc, :], bk, writes=[qx.r])
                    for h in range(4):
                        pb = px[h % 2]
                        rd = rden[h % 2]
                        for mc in range(2):
                            bk = banks[3 + mc]
                            mm_group(bk, (0, TT), [kTm[s].t[:, 2 * h + d, mc * 128:(mc + 1) * 128] for d in range(2)],
                                     [qx.t[:, 2 * h + d, :] for d in range(2)], reads=[kTm[s].r, qx.r])
                            P.op("act", I(A.activation, out=pb.t[:, mc, :], in_=bk[0][:], func=AF.Exp, scale=xscale),
                                 writes=[bk[1], pb.r])
                        if h == 0 and i + 1 < NTL:
                            norm_a(hb[(i + 1) % 3], sq)
                        if h == 2 and i + 1 < NTL:
                            norm_b(hb[(i + 1) % 3], sq, rstd, hn[1 - b], banks[0])
                        if i > 0:
                            wo_part(i - 1, [2 * h, 2 * h + 1])
                        bk = banks[5]
                        mm_group(bk, (0, TT), [ones.t[:], ones.t[:]], [pb.t[:, 0, :], pb.t[:, 1, :]], reads=[pb.r, ones.r])
                        P.op("act", I(A.activation, out=rd.t[:], in_=bk[0][:], func=AF.Ln), writes=[bk[1], rd.r])
                        P.op("act", I(A.activation, out=rd.t[:], in_=rd.t[:], func=AF.Exp, scale=-1.0), writes=[rd.r])
                        for d in range(2):
                            bk = banks[6 + d]
                            mm_group(bk, (0, TT), [vm[s].t[:, mc, (2 * h + d) * 128:(2 * h + d + 1) * 128] for mc in range(2)],
                                     [pb.t[:, mc, :] for mc in range(2)], reads=[pb.r, vm[s].r])
                            P.op("dve", I(V.tensor_tensor, out=ao[b].t[:, 2 * h + d, :], in0=bk[0][:], in1=rd.t[:], op=ALU.mult),
                                 reads=[rd.r], writes=[bk[1], ao[b].r])
                wo_part(NTL - 1, list(range(8)))
                P.barrier()
            dbg_dump()
            if not more():
                break
            with ExitStack() as ph:
                stage = None
                Wu = Buf(ph, nc, "Wu", [128, 8, 2 * DFF], BF16)
                Wd = Buf(ph, nc, "Wd", [128, 16, D], BF16)
                hb = [Buf(ph, nc, "hb%d" % i, [128, 8, TT], F32) for i in range(2)]
                hh = [Buf(ph, nc, "hh%d" % i, [128, 8, 2], F32) for i in range(2)]
                sq = Buf(ph, nc, "sq", [128, 8, TT], BF16)
                sqh = Buf(ph, nc, "sqh", [128, 8, 2], BF16)
                rstd = Buf(ph, nc, "rstd", [128, TT], F32)
                rsth = Buf(ph, nc, "rsth", [128, 2], F32)
                hn = [Buf(ph, nc, "hn%d" % i, [128, 8, TT], BF16) for i in range(2)]
                hnh = [Buf(ph, nc, "hnh%d" % i, [128, 8, 2], BF16) for i in range(2)]
                Gc = [Buf(ph, nc, "Gc%d" % i, [128, TT + 2], F32) for i in range(2)]
                Tc = [Buf(ph, nc, "Tc%d" % i, [128, TT], F32) for i in range(2)]
                Tg = [Buf(ph, nc, "Tg%d" % i, [128, TT], BF16) for i in range(2)]
                aT = [Buf(ph, nc, "aT%d" % i, [128, 16, TT], BF16) for i in range(1)]
                Us = [Buf(ph, nc, "Us%d" % i, [128, TT], BF16) for i in range(2)]
                load_w(ph, Wu, ffn_w_up[layer], 8, 2 * DFF, gain="ffn%d" % layer, stage=stage, order=[0, 4, 1, 5, 2, 6, 3, 7])
                load_w(ph, Wd, ffn_w_down[layer], 16, D, stage=stage)

                def load_tile(i, b):
                    s, ii = divmod(i, NT)
                    t0 = i * TT
                    P.dma("sp", hb[b].t[:], HB3[:, :, t0:t0 + TT], sb=hb[b].r, dram=RB[i])
                    P.op("pool", I(G.memset, hh[b].t[:], 0.0), writes=[hh[b].r])
                    if ii > 0:
                        P.dma("sp", hh[b].t[:, :, 0:1], HB3[:, :, t0 - 1:t0], sb=hh[b].r, dram=RB[i - 1], slow=True)
                    if ii < NT - 1:
                        P.dma("sp", hh[b].t[:, :, 1:2], HB3[:, :, t0 + TT:t0 + TT + 1], sb=hh[b].r, dram=RB[i + 1], slow=True)

                load_tile(0, 0)
                for i in range(2 * NT):
                    b = i % 2
                    t0 = i * TT
                    if i + 1 < 2 * NT:
                        load_tile(i + 1, 1 - b)
                    if i == 0:
                        norm_tile(hb[0], sq, rstd, hn[0], banks[0])
                        norm_tile(hh[0], sqh, rsth, hnh[0], banks[1], n=2)
                    for c in range(16):
                        if c == 10 and i + 1 < 2 * NT:
                            norm_a(hb[1 - b], sq)
                            norm_a(hh[1 - b], sqh, n=2)
                        bg, bu = banks[2 + c % 2], banks[4 + c % 2]
                        gc, tc, tg, us = Gc[c % 2], Tc[c % 2], Tg[c % 2], Us[c % 2]
                        mm_group(bg, (0, TT), [Wu.t[:, k, DFF + c * 128:DFF + (c + 1) * 128] for k in range(8)], [hn[b].t[:, k, :] for k in range(8)], reads=Wu.rc(DFF + c * 128, DFF + (c + 1) * 128) + [hn[b].r])
                        mm_group(banks[1], (8, 10), [Wu.t[:, k, DFF + c * 128:DFF + (c + 1) * 128] for k in range(8)], [hnh[b].t[:, k, :] for k in range(8)], reads=Wu.rc(DFF + c * 128, DFF + (c + 1) * 128) + [hnh[b].r])
                        mm_group(bu, (0, TT), [Wu.t[:, k, c * 128:(c + 1) * 128] for k in range(8)], [hn[b].t[:, k, :] for k in range(8)], reads=Wu.rc(c * 128, (c + 1) * 128) + [hn[b].r])
                        P.op("act", I(A.copy, out=gc.t[:, 1:TT + 1], in_=bg[0][:]), writes=[bg[1], gc.r])
                        P.op("act", I(A.copy, out=us.t[:], in_=bu[0][:]), writes=[bu[1], us.r])
                        P.op("dve", I(V.tensor_copy, out=gc.t[:, 0:TT + 2:TT + 1], in_=banks[1][0][:, 8:10]), writes=[banks[1][1], gc.r])
                        P.op("act", I(A.activation, out=tc.t[:], in_=gc.t[:, 1:TT + 1], func=AF.Identity, scale=vcol("fw%d_1" % layer, c), bias=vcol("fb%d" % layer, c)),
                             reads=[gc.r, vecs.r], writes=[tc.r])
                        P.op("dve", [I(V.scalar_tensor_tensor, out=tc.t[:], in0=gc.t[:, 0:TT], scalar=vcol("fw%d_0" % layer, c), in1=tc.t[:], op0=ALU.mult, op1=ALU.add),
                                     I(V.scalar_tensor_tensor, out=tc.t[:], in0=gc.t[:, 2:TT + 2], scalar=vcol("fw%d_2" % layer, c), in1=tc.t[:], op0=ALU.mult, op1=ALU.add)],
                             reads=[gc.r, vecs.r], writes=[tc.r])
                        P.op("act", I(A.activation, out=tg.t[:], in_=tc.t[:], func=AF.Gelu),
                             reads=[tc.r, vecs.r], writes=[tg.r])
                        P.op("dve", I(V.tensor_tensor, out=aT[0].t[:, c, :], in0=us.t[:], in1=tg.t[:], op=ALU.mult),
                             reads=[tg.r, us.r], writes=[aT[0].r])
                    if i + 1 < 2 * NT:
                        norm_b(hb[1 - b], sq, rstd, hn[1 - b], banks[0])
                        norm_b(hh[1 - b], sqh, rsth, hnh[1 - b], banks[0], n=2, col0=0)
                    for oc in range(8):
                        bk = banks[6 + oc % 2]
                        mm_group(bk, (0, TT), [Wd.t[:, k, oc * 128:(oc + 1) * 128] for k in range(16)], [aT[0].t[:, k, :] for k in range(16)], reads=Wd.rc(oc * 128, (oc + 1) * 128) + [aT[0].r])
                        P.op("dve", I(V.tensor_tensor, out=hb[b].t[:, oc, :], in0=bk[0][:], in1=hb[b].t[:, oc, :], op=ALU.add),
                             writes=[bk[1], hb[b].r])
                    P.dma("pool", HA3[:, :, t0:t0 + TT], hb[b].t[:], sb=hb[b].r, dram=RA[i], load=False)
                P.barrier()
            last_h = [HA3, RA]
            dbg_dump()

        final_norm(last_h[0], last_h[1], outT3, list(range(2 * NT)))
        nops, nwait = P.emit()
        print("program: ops=%d waits=%d" % (nops, nwait))
    return nc


def fmajor(v):
    return np.ascontiguousarray(v.reshape(-1, 128).T)


def make_vecs(inp):
    vecs = np.zeros((128, NV), np.float32)

    def put(name, v):
        a = fmajor(np.asarray(v, np.float32))
        vecs[:, VOFF[name]:VOFF[name] + a.shape[1]] = a

    put("memg", inp["mem_norm_g"])
    for l in range(DEPTH):
        put("mix%d" % l, inp["mix_norm_g"][l])
        put("xa%d" % l, inp["xa_norm_g"][l])
        put("ffn%d" % l, inp["ffn_norm_g"][l])
        for tap in range(3):
            put("fw%d_%d" % (l, tap), inp["ffn_conv_w"][l, tap])
        put("fb%d" % l, inp["ffn_conv_b"][l])
    put("fin", inp["final_norm_g"])
    for j in range(2):
        for tap in range(3):
            put("cc%d_%d" % (j, tap), inp["conv_c"][j, tap])
    return vecs


def make_nab(rpb):
    rpb = np.asarray(rpb, np.float32)
    kr = np.arange(2)[:, None, None, None]
    kc = np.arange(64)[None, :, None, None]
    mi = np.arange(NMI)[None, None, :, None]
    qc = np.arange(64)[None, None, None, :]
    dr = kr - (mi - 6)
    cs = np.clip(qc - 8, 0, 48)
    colv = (kc >= cs) & (kc <= cs + 15)
    dc = kc - qc
    ri = np.clip(dr + 7, 0, 14)
    ci = np.clip(dc + 15, 0, 30)
    ri, ci, colv, dr = np.broadcast_arrays(ri, ci, colv, dr)
    out = np.full((2, 4, 2, 2, 64, NMI, 64), NEG, np.float32)
    for ty in range(2):
        valid = colv & ((dr >= -4) & (dr <= 3) if ty == 0 else (dr >= -7) & (dr <= 7))
        g = rpb[:, :, ri, ci]
        out[:, :, ty] = np.where(valid[None, None], g, np.float32(NEG))
    return np.ascontiguousarray(out.reshape(2, 4, 2, 128, NMI * 64))


_CONST = {}


def consts():
    if not _CONST:
        t = np.arange(S, dtype=np.int64)
        ph = (np.outer(t, t) % S).astype(np.float64) * (2 * np.pi / S)
        _CONST["cosT"] = np.cos(ph).astype(ml_dtypes.bfloat16)
        _CONST["nsinT"] = (-np.sin(ph)).astype(ml_dtypes.bfloat16)
        c = np.arange(128, dtype=np.int64)
        pc = (np.outer(c, c) % 128).astype(np.float64) * (2 * np.pi / 128)
        sc = 1.0 / np.sqrt(float(S) * 128.0)
        _CONST["csc"] = np.concatenate([np.cos(pc) * sc, np.sin(pc) * sc], axis=1).astype(np.float32)
    return _CONST


_NC = {}


def kernel(**inp):
    ncores = 8
    inp = {k: np.asarray(v) for k, v in inp.items()}
    cst = consts()
    vecs = make_vecs(inp)
    nab = make_nab(inp["rpb"])
    shared = {
        "vecs": vecs, "nab": nab, "cosT": cst["cosT"], "nsinT": cst["nsinT"], "csc": cst["csc"],
    }
    for k in ["w_in_ab", "w_out_ab", "w_in_c", "w_out_c", "xa_wq", "xa_wkv", "xa_wo", "ffn_w_up", "ffn_w_down"]:
        shared[k] = np.ascontiguousarray(inp[k], dtype=np.float32)
    in_maps = []
    for c in range(ncores):
        xs = inp["x"][2 * c:2 * c + 2].reshape(NTOK, D)
        ms = inp["mem"][2 * c:2 * c + 2].reshape(NSEQ * NMEM, D)
        m = dict(shared)
        m["xT"] = np.ascontiguousarray(xs.T, dtype=np.float32)
        m["memT"] = np.ascontiguousarray(ms.T, dtype=np.float32)
        in_maps.append(m)
    if "nc" not in _NC:
        _NC["nc"] = build()
    res = run_bass_kernel_spmd(_NC["nc"], in_maps, core_ids=list(range(ncores)))
    out = np.empty((16, S, D), np.float32)
    for c in range(ncores):
        out[2 * c:2 * c + 2] = res.results[c]["outT"].T.reshape(2, S, D)
    return out
```

```python
import numpy as np
import ml_dtypes
from contextlib import ExitStack
import concourse.bass as bass
import concourse.mybir as mybir
from concourse.bass_utils import run_bass_kernel_spmd

F32 = mybir.dt.float32
BF16 = mybir.dt.bfloat16
AF = mybir.ActivationFunctionType
ALU = mybir.AluOpType

D = 1024
S = 4096
NSEQ = 2
NTOK = S * NSEQ
TT = 512
NT = S // TT
DEPTH = 4
NMEM = 256
DFF = 2048
EPS = 1e-6
NEG = -30000.0
NMI = 14


class Res:
    __slots__ = ("name", "writers", "readers", "sem", "cnt")

    def __init__(self, name=""):
        self.name = name
        self.writers = []
        self.readers = []
        self.sem = {}
        self.cnt = {}


class Op:
    __slots__ = ("eng", "isdma", "fns", "deps", "sig", "sem", "val")


class Prog:
    NDSEM = 90
    NHW = 24

    def __init__(self, nc, es):
        self.nc = nc
        self.E = {"pe": nc.tensor, "act": nc.scalar, "dve": nc.vector, "pool": nc.gpsimd, "sp": nc.sync}
        self.ops = []
        self.esem = {e: es.enter_context(nc.semaphore("s_" + e)) for e in ["pe", "act", "dve", "pool"]}
        self.dsem = [es.enter_context(nc.semaphore("d%d" % i)) for i in range(self.NDSEM)]
        self.dcnt = [0] * self.NDSEM
        self.free = {"hw": list(range(self.NHW)), "sw": list(range(self.NHW, self.NDSEM))}
        self.inuse = []
        self.last = {}
        self.dmas = []

    def op(self, eng, fns, reads=(), writes=()):
        if eng != "pe" and isinstance(fns, list) and len(fns) > 1:
            for f in fns:
                o = self.op(eng, f, reads, writes)
            return o
        o = Op()
        o.eng = eng
        o.isdma = False
        o.fns = fns if isinstance(fns, list) else [fns]
        o.deps = []
        o.sig = False
        o.sem = None
        o.val = 0
        for r in reads:
            o.deps += r.writers
            r.readers.append(o)
        for w in writes:
            o.deps += w.writers
            o.deps += w.readers
            w.writers = [o]
            w.readers = []
        self.ops.append(o)
        self.last[eng] = o
        return o

    def dma(self, q, out, in_, sb, dram=None, load=True, slow=False):
        o = Op()
        o.eng = q
        o.isdma = True
        o.deps = []
        o.sig = True
        eng = self.E[q]
        if slow:
            o.fns = [lambda: eng.dma_start(out=out, in_=in_, allow_slow_non_contiguous=True)]
        else:
            o.fns = [lambda: eng.dma_start(out=out, in_=in_)]
        kind = "sw" if q == "pool" else "hw"
        if kind not in sb.sem:
            idx = self.free[kind].pop()
            sb.sem[kind] = idx
            sb.cnt[kind] = self.dcnt[idx]
            self.inuse.append((sb, kind))
        if load:
            if sb.readers or any(not w.isdma for w in sb.writers):
                o.deps += sb.readers + sb.writers
                sb.writers = [o]
                sb.readers = []
            else:
                sb.writers.append(o)
            if dram is not None:
                o.deps += dram.writers
                dram.readers.append(o)
        else:
            o.deps += sb.writers
            sb.readers.append(o)
            if dram is not None:
                o.deps += dram.writers + dram.readers
                dram.writers = [o]
                dram.readers = []
        sb.cnt[kind] += 16
        o.sem = self.dsem[sb.sem[kind]]
        o.val = sb.cnt[kind]
        self.ops.append(o)
        self.dmas.append(o)
        return o

    def barrier(self):
        lasts = list(self.last.values())
        dm = list(self.dmas)
        for e in ["pe", "act", "dve", "pool", "sp"]:
            o = Op()
            o.eng = e
            o.isdma = False
            o.fns = []
            o.deps = lasts + dm
            o.sig = False
            o.sem = None
            o.val = 0
            self.ops.append(o)
        self.dmas = []
        for r, kind in self.inuse:
            idx = r.sem.pop(kind)
            self.dcnt[idx] = r.cnt.pop(kind)
            self.free[kind].append(idx)
        self.inuse = []

    def emit(self):
        for o in self.ops:
            for d in o.deps:
                if d is o or d.isdma:
                    continue
                if (not o.isdma) and d.eng == o.eng and o.eng == "pe":
                    continue
                d.sig = True
        cnt = {e: 0 for e in self.esem}
        for o in self.ops:
            if (not o.isdma) and o.sig:
                cnt[o.eng] += 1
                o.sem = self.esem[o.eng]
                o.val = cnt[o.eng]
        seen = {e: {} for e in self.E}
        nwait = 0
        for o in self.ops:
            eng = self.E[o.eng]
            need = {}
            for d in o.deps:
                if d is o:
                    continue
                if (not d.isdma) and (not o.isdma) and d.eng == o.eng and o.eng == "pe":
                    continue
                k = id(d.sem)
                if k not in need or need[k][1] < d.val:
                    need[k] = (d.sem, d.val)
            sn = seen[o.eng]
            for k, (sem, val) in need.items():
                if sn.get(k, 0) < val:
                    eng.wait_ge(sem, val)
                    sn[k] = val
                    nwait += 1
            n = len(o.fns)
            for i, f in enumerate(o.fns):
                ins = f()
                if i == n - 1 and o.sig:
                    ins.then_inc(o.sem, 16 if o.isdma else 1)
        return len(self.ops), nwait


def I(fn, *a, **kw):
    return lambda: fn(*a, **kw)


class Buf:
    CNT = [0]

    def __init__(self, es, nc, name, shape, dtype):
        Buf.CNT[0] += 1
        name = "%s_%d" % (name, Buf.CNT[0])
        self.t = es.enter_context(nc.sbuf_tensor(name, shape, dtype))
        self.r = Res(name)
        self.rk = None
        self.pw = None
        self.K = None

    def rc(self, c0, c1):
        p0, p1 = c0 // self.pw, (c1 - 1) // self.pw
        return [self.rk[(k, p)] for k in range(self.K) for p in range(p0, p1 + 1)]


def vec_layout():
    off = {}
    cur = [0]

    def add(name, n):
        off[name] = cur[0]
        cur[0] += n

    add("memg", 8)
    for l in range(DEPTH):
        add("mix%d" % l, 8)
        add("xa%d" % l, 8)
        add("ffn%d" % l, 8)
        for tap in range(3):
            add("fw%d_%d" % (l, tap), 16)
        add("fb%d" % l, 16)
    add("fin", 8)
    for j in range(2):
        for tap in range(3):
            add("cc%d_%d" % (j, tap), 8)
    return off, cur[0]


VOFF, NV = vec_layout()


def build(nphase=100, debug=False):
    nc = bass.Bass("TRN2", target_bir_lowering=False)

    def din(name, shape, dt=F32):
        return nc.dram_tensor(name, shape, dt, kind="ExternalInput").ap()

    xT = din("xT", [D, NTOK])
    memT = din("memT", [D, NSEQ * NMEM])
    vecs_d = din("vecs", [128, NV])
    w_in_ab = din("w_in_ab", [2, D, 2048])
    w_out_ab = din("w_out_ab", [2, D, D])
    w_in_c = din("w_in_c", [2, D, 3 * D])
    w_out_c = din("w_out_c", [2, D, D])
    xa_wq = din("xa_wq", [DEPTH, D, D])
    xa_wkv = din("xa_wkv", [DEPTH, D, 2 * D])
    xa_wo = din("xa_wo", [DEPTH, D, D])
    ffn_w_up = din("ffn_w_up", [DEPTH, D, 2 * DFF])
    ffn_w_down = din("ffn_w_down", [DEPTH, DFF, D])
    cosT = din("cosT", [S, S], BF16)
    nsinT = din("nsinT", [S, S], BF16)
    csc_d = din("csc", [128, 256])
    nab_d = din("nab", [2, 4, 2, 128, NMI * 64])
    outT = nc.dram_tensor("outT", [D, NTOK], F32, kind="ExternalOutput").ap()
    HA = nc.dram_tensor("HA", [D, NTOK], F32).ap()
    HB = nc.dram_tensor("HB", [D, NTOK], F32).ap()

    def fm(ap):
        return ap.rearrange("(k p) t -> p k t", p=128)

    xT3, HA3, HB3, outT3 = fm(xT), fm(HA), fm(HB), fm(outT)
    RX = [None] * 16
    RA = [Res("HA%d" % i) for i in range(16)]
    RB = [Res("HB%d" % i) for i in range(16)]

    es = ExitStack()
    with es:
        P = Prog(nc, es)
        V, G, A, T, PL = nc.vector, nc.gpsimd, nc.scalar, nc.tensor, nc.gpsimd
        banks = []
        for i in range(8):
            b = es.enter_context(nc.psum_tensor("bank%d" % i, [128, 512], F32))
            banks.append((b, Res("bank%d" % i)))
        vecs = Buf(es, nc, "vecs", [128, NV], F32)
        ones = Buf(es, nc, "ones", [128, 128], BF16)
        mT = [Buf(es, nc, "mT%d" % s, [128, 8, NMEM], BF16) for s in range(NSEQ)]
        P.dma("sp", vecs.t[:], vecs_d, sb=vecs.r)
        P.op("pool", I(G.memset, ones.t[:], 1.0), writes=[ones.r])
        epsb = Buf(es, nc, "epsb", [128, 1], F32)
        P.op("pool", I(G.memset, epsb.t[:], EPS), writes=[epsb.r])

        def vcol(name, k=0, n=1):
            o = VOFF[name] + k
            return vecs.t[:, o:o + n]

        cvt_rr = [0]

        def load_w(es2, dst, src, K, N, gain=None, stage=None, cols=None, pw=None, order=None):
            c0, c1 = cols if cols is not None else (0, N)
            src3 = src.rearrange("(k p) n -> p k n", p=128)
            dst.pw, dst.K = N, K
            dst.rk = {(k, 0): Res("w") for k in range(K)}
            for k in range(K):
                P.dma("pool", dst.t[:, k, :], src3[:, k, c0:c1], sb=dst.rk[(k, 0)])
            if gain is not None:
                for k in range(K):
                    e = ["dve", "act"][cvt_rr[0] % 2]
                    cvt_rr[0] += 1
                    g_ap = vcol(gain, k)
                    w_ap = dst.t[:, k, :]
                    if e == "act":
                        f = I(A.activation, out=w_ap, in_=w_ap, func=AF.Copy, scale=g_ap)
                    else:
                        f = I(V.tensor_scalar, out=w_ap, in0=w_ap, scalar1=g_ap, scalar2=None, op0=ALU.mult)
                    P.op(e, f, reads=[vecs.r], writes=[dst.rk[(k, 0)]])

        evac_rr = [0]

        def evac(out_ap, bank, reads=(), writes=()):
            bt, br = bank
            e = ["act", "dve"][evac_rr[0] % 2]
            evac_rr[0] += 1
            if len(out_ap.shape) == 2:
                src = bt[:, 0:out_ap.shape[-1]]
            else:
                src = bt[:].rearrange("p (g n) -> p g n", g=out_ap.shape[1])
            if e == "act":
                P.op("act", I(A.copy, out=out_ap, in_=src), reads=list(reads), writes=[br] + list(writes))
            else:
                P.op("dve", I(V.tensor_copy, out=out_ap, in_=src), reads=list(reads), writes=[br] + list(writes))

        def norm_a(hb, sq, n=TT):
            P.op("act", I(A.activation, out=sq.t[:, :, 0:n], in_=hb.t[:, :, 0:n], func=AF.Square), reads=[hb.r], writes=[sq.r])

        def norm_b(hb, sq, rstd, hn, ssbank, n=TT, col0=0):
            bt, br = ssbank
            P.op("pe", [I(T.matmul, bt[:, col0:col0 + n], ones.t[:], sq.t[:, k, 0:n], start=(k == 0), stop=(k == 7)) for k in range(8)],
                 reads=[sq.r, ones.r], writes=[br])
            P.op("act", I(A.activation, out=rstd.t[:, 0:n], in_=bt[:, col0:col0 + n], func=AF.Ln, scale=1.0 / D, bias=epsb.t[:, 0:1]),
                 reads=[epsb.r], writes=[br, rstd.r])
            P.op("act", I(A.activation, out=rstd.t[:, 0:n], in_=rstd.t[:, 0:n], func=AF.Exp, scale=-0.5), writes=[rstd.r])
            P.op("dve", I(V.tensor_tensor, out=hn.t[:, :, 0:n], in0=hb.t[:, :, 0:n],
                                                 in1=rstd.t[:, 0:n].unsqueeze(1).broadcast_to([128, 8, n]), op=ALU.mult),
                 reads=[hb.r, rstd.r], writes=[hn.r])

        def norm_tile(hb, sq, rstd, hn, ssbank, n=TT):
            norm_a(hb, sq, n)
            norm_b(hb, sq, rstd, hn, ssbank, n)

        def mm_group(bank, cols, lhs_list, rhs_list, reads, start=True, stop=True):
            bt, br = bank
            n = len(lhs_list)
            fns = [I(T.matmul, bt[:, cols[0]:cols[1]], lhs_list[i], rhs_list[i], start=(start and i == 0), stop=(stop and i == n - 1))
                   for i in range(n)]
            P.op("pe", fns, reads=reads, writes=[br])

        with ExitStack() as ph:
            mem3 = fm(memT)
            mb = Buf(ph, nc, "mb", [128, 8, NMEM], F32)
            msq = Buf(ph, nc, "msq", [128, 8, NMEM], BF16)
            mrs = Buf(ph, nc, "mrs", [128, NMEM], F32)
            for s in range(NSEQ):
                P.dma("sp", mb.t[:], mem3[:, :, s * NMEM:(s + 1) * NMEM], sb=mb.r)
                norm_tile(mb, msq, mrs, mT[s], banks[0], n=NMEM)
            P.barrier()

        def final_norm(Hf3, Rf, out3, tiles):
            with ExitStack() as ph:
                hb = [Buf(ph, nc, "hb%d" % i, [128, 8, TT], F32) for i in range(2)]
                ob = [Buf(ph, nc, "ob%d" % i, [128, 8, TT], F32) for i in range(2)]
                sq = Buf(ph, nc, "sq", [128, 8, TT], BF16)
                rstd = Buf(ph, nc, "rstd", [128, TT], F32)
                for n_, i in enumerate(tiles):
                    b = n_ % 2
                    t0 = i * TT
                    o0 = (n_ if out3 is not outT3 else i) * TT
                    P.dma("sp", hb[b].t[:], Hf3[:, :, t0:t0 + TT], sb=hb[b].r, dram=Rf[i])
                    bt, br = banks[n_ % 2]
                    P.op("act", I(A.activation, out=sq.t[:], in_=hb[b].t[:], func=AF.Square), reads=[hb[b].r], writes=[sq.r])
                    P.op("pe", [I(T.matmul, bt[:], ones.t[:], sq.t[:, k, :], start=(k == 0), stop=(k == 7)) for k in range(8)],
                         reads=[sq.r, ones.r], writes=[br])
                    P.op("act", I(A.activation, out=rstd.t[:], in_=bt[:], func=AF.Sqrt, scale=1.0 / D, bias=epsb.t[:, 0:1]),
                         reads=[epsb.r], writes=[br, rstd.r])
                    P.op("dve", I(V.reciprocal, out=rstd.t[:], in_=rstd.t[:]), writes=[rstd.r])
                    P.op("dve", [I(V.scalar_tensor_tensor, out=ob[b].t[:, k, :], in0=hb[b].t[:, k, :], scalar=vcol("fin", k), in1=rstd.t[:], op0=ALU.mult, op1=ALU.mult)
                                 for k in range(8)], reads=[hb[b].r, rstd.r, vecs.r], writes=[ob[b].r])
                    P.dma("pool", out3[:, :, o0:o0 + TT], ob[b].t[:], sb=ob[b].r, load=False)
                P.barrier()

        def dbg_dump():
            if debug:
                k = phase[0]
                d3 = fm(nc.dram_tensor("dbg%d" % k, [D, NTOK], F32, kind="ExternalOutput").ap())
                Hf3, Rf = last_h
                with ExitStack() as ph:
                    hb = [Buf(ph, nc, "hb%d" % i, [128, 8, TT], F32) for i in range(2)]
                    for i in range(2 * NT):
                        b = i % 2
                        P.dma("sp", hb[b].t[:], Hf3[:, :, i * TT:(i + 1) * TT], sb=hb[b].r, dram=Rf[i])
                        P.dma("pool", d3[:, :, i * TT:(i + 1) * TT], hb[b].t[:], sb=hb[b].r, load=False)
                    P.barrier()

        phase = [0]

        def more():
            phase[0] += 1
            return phase[0] <= nphase

        last_h = [xT3, RX]

        for layer in range(DEPTH):
            j = layer // 2
            Hin3, Rin = (xT3, RX) if layer == 0 else (HA3, RA)
            if not more():
                break
            if layer % 2 == 0:
                for s in range(NSEQ):
                    tb = s * S
                    with ExitStack() as ph:
                        Pb = Buf(ph, nc, "Pb", [128, 32, 4, 256], BF16)
                        stage = None
                        with ExitStack() as p1:
                            Wz = Buf(p1, nc, "Wz", [128, 8, 512], BF16)
                            csc = Buf(p1, nc, "cscb", [128, 256], BF16)
                            cscf = Buf(p1, nc, "cscf", [128, 256], F32)
                            hb = [Buf(p1, nc, "hb%d" % i, [128, 8, TT], F32) for i in range(2)]
                            sq = Buf(p1, nc, "sq", [128, 8, TT], BF16)
                            rstd = Buf(p1, nc, "rstd", [128, TT], F32)
                            hn = [Buf(p1, nc, "hn%d" % i, [128, 8, TT], BF16) for i in range(2)]
                            zf = [Buf(p1, nc, "zf%d" % i, [128, 4, TT], BF16) for i in range(2)]
                            P.dma("sp", cscf.t[:], csc_d, sb=cscf.r)
                            P.op("dve", I(V.tensor_copy, out=csc.t[:], in_=cscf.t[:]), reads=[cscf.r], writes=[csc.r])
                            load_w(p1, Wz, w_in_ab[j], 8, 512, gain="mix%d" % layer, stage=stage, cols=(0, 512))
                            P.dma("sp", hb[0].t[:], Hin3[:, :, tb:tb + TT], sb=hb[0].r, dram=Rin[s * NT])
                            for i in range(NT):
                                b = i % 2
                                if i + 1 < NT:
                                    t1 = tb + (i + 1) * TT
                                    P.dma("sp", hb[1 - b].t[:], Hin3[:, :, t1:t1 + TT], sb=hb[1 - b].r, dram=Rin[s * NT + i + 1])
                                if i == 0:
                                    norm_tile(hb[0], sq, rstd, hn[0], banks[0])
                                if i + 1 < NT:
                                    norm_a(hb[1 - b], sq)
                                for g in range(4):
                                    bk = banks[1 + g % 2]
                                    mm_group(bk, (0, TT), [Wz.t[:, k, g * 128:(g + 1) * 128] for k in range(8)],
                                             [hn[b].t[:, k, :] for k in range(8)], reads=Wz.rc(g * 128, (g + 1) * 128) + [hn[b].r])
                                    evac(zf[b].t[:, g, :], bk, writes=[zf[b].r])
                                if i + 1 < NT:
                                    norm_b(hb[1 - b], sq, rstd, hn[1 - b], banks[0])
                                for c4 in range(4):
                                    for gp in range(2):
                                        bk = banks[3 + (c4 * 2 + gp) % 4]
                                        for gi in range(2):
                                            g = gp * 2 + gi
                                            mm_group(bk, (gi * 256, gi * 256 + 256), [zf[b].t[:, g, c4 * 128:(c4 + 1) * 128]], [csc.t[:]],
                                                     reads=[zf[b].r, csc.r])
                                        evac(Pb.t[:, i * 4 + c4, gp * 2:gp * 2 + 2, :], bk, writes=[Pb.r])
                            P.barrier()
                        with ExitStack() as p2:
                            Wo = Buf(p2, nc, "Wo", [128, 4, D], BF16)
                            tab = [Buf(p2, nc, "tab%d" % i, [128, 2, 8, TT], BF16) for i in range(2)]
                            YT = [Buf(p2, nc, "YT%d" % i, [128, 4, TT], BF16) for i in range(2)]
                            hb = [Buf(p2, nc, "hc%d" % i, [128, 8, TT], F32) for i in range(2)]
                            load_w(p2, Wo, w_out_ab[j][0:512, :], 4, D, stage=stage)
                            cos3 = cosT.rearrange("(a p) t -> p a t", p=128)
                            sin3 = nsinT.rearrange("(a p) t -> p a t", p=128)
                            tcnt = 0
                            for i in range(NT):
                                b = i % 2
                                t0 = tb + i * TT
                                P.dma("sp", hb[b].t[:], Hin3[:, :, t0:t0 + TT], sb=hb[b].r, dram=Rin[s * NT + i])
                                for q in range(4):
                                    tbf = tab[tcnt % 2]
                                    tcnt += 1
                                    P.dma("sp", tbf.t[:, 0, :, :], cos3[:, q * 8:(q + 1) * 8, i * TT:(i + 1) * TT], sb=tbf.r)
                                    P.dma("sp", tbf.t[:, 1, :, :], sin3[:, q * 8:(q + 1) * 8, i * TT:(i + 1) * TT], sb=tbf.r)
                                    for g in range(4):
                                        lhs, rhs = [], []
                                        for a in range(8):
                                            lhs.append(Pb.t[:, q * 8 + a, g, 0:128])
                                            rhs.append(tbf.t[:, 0, a, :])
                                            lhs.append(Pb.t[:, q * 8 + a, g, 128:256])
                                            rhs.append(tbf.t[:, 1, a, :])
                                        mm_group(banks[g], (0, TT), lhs, rhs, reads=[Pb.r, tbf.r], start=(q == 0), stop=(q == 3))
                                for g in range(4):
                                    evac(YT[b].t[:, g, :], banks[g], writes=[YT[b].r])
                                for oc in range(8):
                                    bk = banks[4 + oc % 4]
                                    mm_group(bk, (0, TT), [Wo.t[:, g, oc * 128:(oc + 1) * 128] for g in range(4)],
                                             [YT[b].t[:, g, :] for g in range(4)], reads=Wo.rc(oc * 128, (oc + 1) * 128) + [YT[b].r])
                                    P.op("dve", I(V.tensor_tensor, out=hb[b].t[:, oc, :], in0=bk[0][:], in1=hb[b].t[:, oc, :], op=ALU.add),
                                         writes=[bk[1], hb[b].r])
                                P.dma("pool", HB3[:, :, t0:t0 + TT], hb[b].t[:], sb=hb[b].r, dram=RB[s * NT + i], load=False)
                            P.barrier()
                    with ExitStack() as ph:
                        qT = Buf(ph, nc, "qT", [128, 4, S], BF16)
                        kT = Buf(ph, nc, "kT", [128, 4, S], BF16)
                        vv = Buf(ph, nc, "vv", [128, 32, 512], BF16)
                        stage = None
                        with ExitStack() as p1:
                            Wq = Buf(p1, nc, "Wqkv", [128, 8, 1536], BF16)
                            hb = [Buf(p1, nc, "hb%d" % i, [128, 8, TT], F32) for i in range(2)]
                            sq = Buf(p1, nc, "sq", [128, 8, TT], BF16)
                            rstd = Buf(p1, nc, "rstd", [128, TT], F32)
                            hn = [Buf(p1, nc, "hn%d" % i, [128, 8, TT], BF16) for i in range(2)]
                            load_w(p1, Wq, w_in_ab[j], 8, 1536, gain="mix%d" % layer, stage=stage, cols=(512, 2048))
                            P.dma("sp", hb[0].t[:], Hin3[:, :, tb:tb + TT], sb=hb[0].r, dram=Rin[s * NT])
                            for i in range(NT):
                                b = i % 2
                                if i + 1 < NT:
                                    t1 = tb + (i + 1) * TT
                                    P.dma("sp", hb[1 - b].t[:], Hin3[:, :, t1:t1 + TT], sb=hb[1 - b].r, dram=Rin[s * NT + i + 1])
                                if i == 0:
                                    norm_tile(hb[0], sq, rstd, hn[0], banks[0])
                                if i + 1 < NT:
                                    norm_a(hb[1 - b], sq)
                                nb = 0
                                for dst, c0 in ((qT, 0), (kT, 512)):
                                    for h in range(4):
                                        bk = banks[1 + nb % 7]
                                        nb += 1
                                        mm_group(bk, (0, TT), [Wq.t[:, k, c0 + h * 128:c0 + (h + 1) * 128] for k in range(8)],
                                                 [hn[b].t[:, k, :] for k in range(8)], reads=Wq.rc(c0 + h * 128, c0 + (h + 1) * 128) + [hn[b].r])
                                        evac(dst.t[:, h, i * TT:(i + 1) * TT], bk, writes=[dst.r])
                                if i + 1 < NT:
                                    norm_b(hb[1 - b], sq, rstd, hn[1 - b], banks[0])
                                for c4 in range(4):
                                    bk = banks[1 + nb % 7]
                                    nb += 1
                                    mm_group(bk, (0, 512), [hn[b].t[:, k, c4 * 128:(c4 + 1) * 128] for k in range(8)],
                                             [Wq.t[:, k, 1024:1536] for k in range(8)], reads=Wq.rc(1024, 1536) + [hn[b].r])
                                    evac(vv.t[:, i * 4 + c4, :], bk, writes=[vv.r])
                            P.barrier()
                        with ExitStack() as p2:
                            Ei = Buf(p2, nc, "Ei", [128, 4, 2, NMI * 64], BF16)
                            nst = [Buf(p2, nc, "nst%d" % i, [128, NMI * 64], F32) for i in range(2)]
                            Wo = Buf(p2, nc, "Wo2", [128, 4, D], BF16)
                            Pt = [Buf(p2, nc, "Pt%d" % i, [128, 6, 256], BF16) for i in range(3)]
                            ya = [Buf(p2, nc, "ya%d" % i, [128, 4, TT], BF16) for i in range(2)]
                            rden = [Buf(p2, nc, "rden%d" % i, [128, 256], F32) for i in range(3)]
                            hb = [Buf(p2, nc, "hc%d" % i, [128, 8, TT], F32) for i in range(2)]
                            load_w(p2, Wo, w_out_ab[j][512:1024, :], 4, D, stage=stage)
                            n_ = 0
                            for h in range(4):
                                for ty in range(2):
                                    st = nst[n_ % 2]
                                    n_ += 1
                                    P.dma("sp", st.t[:], nab_d[j, h, ty], sb=st.r)
                                    P.op("act", I(A.activation, out=Ei.t[:, h, ty, :], in_=st.t[:], func=AF.Exp),
                                         reads=[st.r], writes=[Ei.r])
                            scale = 128.0 ** -0.5
                            blocks = [(i, h, bl) for i in range(NT) for h in range(4) for bl in range(2)]

                            def chunks_of(blk):
                                if blk == 0:
                                    return [(c, c, 1, 6 - 2 * c) for c in range(4)]
                                if blk == 15:
                                    return [(c, 28 + c, 1, 10 - 2 * c) for c in range(4)]
                                return [(c, 2 * blk - 2 + c, 0, 10 - 2 * c) for c in range(6)]

                            def issue_S(n):
                                i, h, bl = blocks[n]
                                blk = 2 * i + bl
                                ch = chunks_of(blk)
                                set_ = n % 2
                                pt = Pt[n % 3]
                                for pr in range(len(ch) // 2):
                                    bk = banks[set_ * 3 + pr]
                                    for ci in range(2):
                                        c, kci, ty, mi0 = ch[pr * 2 + ci]
                                        mm_group(bk, (ci * 256, ci * 256 + 256), [kT.t[:, h, kci * 128:(kci + 1) * 128]],
                                                 [qT.t[:, h, blk * 256:(blk + 1) * 256]], reads=[kT.r, qT.r])
                                    P.op("act", I(A.activation,
                                        out=pt.t[:, 2 * pr:2 * pr + 2, :], in_=bk[0][:].rearrange("p (c n) -> p c n", c=2), func=AF.Exp, scale=scale),
                                        writes=[bk[1], pt.r])
                                    for ci in range(2):
                                        c, kci, ty, mi0 = ch[pr * 2 + ci]
                                        e = "dve"
                                        eng = V if e == "dve" else G
                                        P.op(e, I(eng.tensor_tensor,
                                            out=pt.t[:, c, :], in0=pt.t[:, c, :], in1=Ei.t[:, h, ty, mi0 * 64:mi0 * 64 + 256], op=ALU.mult),
                                            reads=[Ei.r], writes=[pt.r])

                            def issue_DO(n):
                                i, h, bl = blocks[n]
                                blk = 2 * i + bl
                                ch = chunks_of(blk)
                                pt = Pt[n % 3]
                                bk = banks[6 + n % 2]
                                mm_group(bk, (0, 256), [ones.t[:] for _ in ch], [pt.t[:, c, :] for (c, _, _, _) in ch], reads=[pt.r, ones.r])
                                mm_group(bk, (256, 512), [vv.t[:, kci, h * 128:(h + 1) * 128] for (_, kci, _, _) in ch],
                                         [pt.t[:, c, :] for (c, _, _, _) in ch], reads=[pt.r, vv.r])
                                rd = rden[n % 3]
                                yb = ya[i % 2]
                                P.op("act", I(A.activation, out=rd.t[:], in_=bk[0][:, 0:256], func=AF.Ln), writes=[bk[1], rd.r])
                                P.op("act", I(A.activation, out=rd.t[:], in_=rd.t[:], func=AF.Exp, scale=-1.0), writes=[rd.r])
                                P.op("dve", I(V.tensor_tensor, out=yb.t[:, h, bl * 256:(bl + 1) * 256], in0=bk[0][:, 256:512], in1=rd.t[:], op=ALU.mult),
                                     reads=[rd.r], writes=[bk[1], yb.r])

                            issue_S(0)
                            issue_S(1)
                            for n in range(len(blocks)):
                                i, h, bl = blocks[n]
                                b = i % 2
                                t0 = tb + i * TT
                                if h == 0 and bl == 0:
                                    P.dma("sp", hb[b].t[:], HB3[:, :, t0:t0 + TT], sb=hb[b].r, dram=RB[s * NT + i])
                                if n + 2 < len(blocks):
                                    issue_S(n + 2)
                                issue_DO(n)
                                if h == 3 and bl == 1:
                                    for oc in range(8):
                                        bk = banks[7]
                                        mm_group(bk, (0, TT), [Wo.t[:, hh, oc * 128:(oc + 1) * 128] for hh in range(4)],
                                                 [ya[b].t[:, hh, :] for hh in range(4)], reads=Wo.rc(oc * 128, (oc + 1) * 128) + [ya[b].r])
                                        P.op("dve", I(V.tensor_tensor, out=hb[b].t[:, oc, :], in0=bk[0][:], in1=hb[b].t[:, oc, :], op=ALU.add),
                                             writes=[bk[1], hb[b].r])
                                    P.dma("pool", HB3[:, :, t0:t0 + TT], hb[b].t[:], sb=hb[b].r, dram=RB[s * NT + i], load=False)
                            P.barrier()
            else:
                with ExitStack() as ph:
                    stage = None
                    Wi = Buf(ph, nc, "Wi", [128, 8, 3 * D], BF16)
                    Wo = Buf(ph, nc, "Woc", [128, 8, D], BF16)
                    hb = [Buf(ph, nc, "hb%d" % i, [128, 8, TT], F32) for i in range(2)]
                    hh = [Buf(ph, nc, "hh%d" % i, [128, 8, 2], F32) for i in range(2)]
                    sq = Buf(ph, nc, "sq", [128, 8, TT], BF16)
                    sqh = Buf(ph, nc, "sqh", [128, 8, 2], BF16)
                    rstd = Buf(ph, nc, "rstd", [128, TT], F32)
                    rsth = Buf(ph, nc, "rsth", [128, 2], F32)
                    hn = [Buf(ph, nc, "hn%d" % i, [128, 8, TT], BF16) for i in range(2)]
                    hnh = [Buf(ph, nc, "hnh%d" % i, [128, 8, 2], BF16) for i in range(2)]
                    Wc = [Buf(ph, nc, "Wc%d" % i, [128, TT + 2], F32) for i in range(2)]
                    Tc = [Buf(ph, nc, "Tc%d" % i, [128, TT], F32) for i in range(2)]
                    Uh = [Buf(ph, nc, "Uh%d" % i, [128, 2], F32) for i in range(2)]
                    Bs = [Buf(ph, nc, "Bs%d" % i, [128, TT], BF16) for i in range(2)]
                    yT = [Buf(ph, nc, "yT%d" % i, [128, 8, TT], BF16) for i in range(2)]
                    load_w(ph, Wi, w_in_c[j], 8, 3 * D, gain="mix%d" % layer, stage=stage, order=[0, 2, 4, 1, 3, 5])
                    load_w(ph, Wo, w_out_c[j], 8, D, stage=stage)

                    def load_tile(i, b):
                        s, ii = divmod(i, NT)
                        t0 = i * TT
                        P.dma("sp", hb[b].t[:], Hin3[:, :, t0:t0 + TT], sb=hb[b].r, dram=Rin[i])
                        P.op("pool", I(G.memset, hh[b].t[:], 0.0), writes=[hh[b].r])
                        if ii > 0:
                            P.dma("sp", hh[b].t[:, :, 0:1], Hin3[:, :, t0 - 1:t0], sb=hh[b].r, dram=Rin[i - 1], slow=True)
                        if ii < NT - 1:
                            P.dma("sp", hh[b].t[:, :, 1:2], Hin3[:, :, t0 + TT:t0 + TT + 1], sb=hh[b].r, dram=Rin[i + 1], slow=True)

                    load_tile(0, 0)
                    for i in range(2 * NT):
                        b = i % 2
                        t0 = i * TT
                        if i + 1 < 2 * NT:
                            load_tile(i + 1, 1 - b)
                        if i == 0:
                            norm_tile(hb[0], sq, rstd, hn[0], banks[0])
                            norm_tile(hh[0], sqh, rsth, hnh[0], banks[1], n=2)
                        for c in range(8):
                            if c == 5 and i + 1 < 2 * NT:
                                norm_a(hb[1 - b], sq)
                                norm_a(hh[1 - b], sqh, n=2)
                            bc, bu, bb_ = banks[2 + c % 2], banks[4 + c % 2], banks[6 + c % 2]
                            wc, tc = Wc[c % 2], Tc[c % 2]
                            mm_group(bc, (0, TT), [Wi.t[:, k, D + c * 128:D + (c + 1) * 128] for k in range(8)], [hn[b].t[:, k, :] for k in range(8)], reads=Wi.rc(D + c * 128, D + (c + 1) * 128) + [hn[b].r])
                            mm_group(banks[1], (8, 10), [Wi.t[:, k, D + c * 128:D + (c + 1) * 128] for k in range(8)], [hnh[b].t[:, k, :] for k in range(8)], reads=Wi.rc(D + c * 128, D + (c + 1) * 128) + [hnh[b].r])
                            mm_group(banks[1], (16, 18), [Wi.t[:, k, 2 * D + c * 128:2 * D + (c + 1) * 128] for k in range(8)], [hnh[b].t[:, k, :] for k in range(8)], reads=Wi.rc(2 * D + c * 128, 2 * D + (c + 1) * 128) + [hnh[b].r])
                            mm_group(bu, (0, TT), [Wi.t[:, k, 2 * D + c * 128:2 * D + (c + 1) * 128] for k in range(8)], [hn[b].t[:, k, :] for k in range(8)], reads=Wi.rc(2 * D + c * 128, 2 * D + (c + 1) * 128) + [hn[b].r])
                            mm_group(bb_, (0, TT), [Wi.t[:, k, c * 128:(c + 1) * 128] for k in range(8)], [hn[b].t[:, k, :] for k in range(8)], reads=Wi.rc(c * 128, (c + 1) * 128) + [hn[b].r])
                            P.op("act", I(A.copy, out=wc.t[:, 1:TT + 1], in_=bc[0][:]), writes=[bc[1], wc.r])
                            bs = Bs[c % 2]
                            P.op("act", I(A.copy, out=bs.t[:], in_=bb_[0][:]), writes=[bb_[1], bs.r])
                            uh = Uh[c % 2]
                            P.op("act", I(A.copy, out=uh.t[:], in_=banks[1][0][:, 16:18]), writes=[banks[1][1], uh.r])
                            P.op("dve", I(V.tensor_tensor, out=wc.t[:, 0:TT + 2:TT + 1], in0=banks[1][0][:, 8:10], in1=uh.t[:], op=ALU.mult),
                                 reads=[uh.r], writes=[banks[1][1], wc.r])
                            P.op("dve", I(V.tensor_tensor, out=wc.t[:, 1:TT + 1], in0=bu[0][:], in1=wc.t[:, 1:TT + 1], op=ALU.mult),
                                 writes=[bu[1], wc.r])
                            P.op("act", I(A.activation, out=tc.t[:], in_=wc.t[:, 1:TT + 1], func=AF.Copy, scale=vcol("cc%d_1" % j, c)),
                                 reads=[wc.r, vecs.r], writes=[tc.r])
                            P.op("dve", [I(V.scalar_tensor_tensor, out=tc.t[:], in0=wc.t[:, 0:TT], scalar=vcol("cc%d_0" % j, c), in1=tc.t[:], op0=ALU.mult, op1=ALU.add),
                                         I(V.scalar_tensor_tensor, out=tc.t[:], in0=wc.t[:, 2:TT + 2], scalar=vcol("cc%d_2" % j, c), in1=tc.t[:], op0=ALU.mult, op1=ALU.add)],
                                 reads=[wc.r, vecs.r], writes=[tc.r])
                            P.op("dve", I(V.tensor_tensor, out=yT[b].t[:, c, :], in0=bs.t[:], in1=tc.t[:], op=ALU.mult),
                                 reads=[tc.r, bs.r], writes=[yT[b].r])
                        if i + 1 < 2 * NT:
                            norm_b(hb[1 - b], sq, rstd, hn[1 - b], banks[0])
                            norm_b(hh[1 - b], sqh, rsth, hnh[1 - b], banks[0], n=2, col0=0)
                        for oc in range(8):
                            bk = banks[2 + oc % 6]
                            mm_group(bk, (0, TT), [Wo.t[:, k, oc * 128:(oc + 1) * 128] for k in range(8)], [yT[b].t[:, k, :] for k in range(8)], reads=Wo.rc(oc * 128, (oc + 1) * 128) + [yT[b].r])
                            P.op("dve", I(V.tensor_tensor, out=hb[b].t[:, oc, :], in0=bk[0][:], in1=hb[b].t[:, oc, :], op=ALU.add),
                                 writes=[bk[1], hb[b].r])
                        P.dma("pool", HB3[:, :, t0:t0 + TT], hb[b].t[:], sb=hb[b].r, dram=RB[i], load=False)
                    P.barrier()
            last_h = [HB3, RB]
            dbg_dump()
            if not more():
                break
            with ExitStack() as ph:
                stage = None
                Wq = Buf(ph, nc, "xWq", [128, 8, D], BF16)
                Wo = Buf(ph, nc, "xWo", [128, 8, D], BF16)
                kTm = [Buf(ph, nc, "kTm%d" % s, [128, 8, NMEM], BF16) for s in range(NSEQ)]
                vm = [Buf(ph, nc, "vm%d" % s, [128, 2, D], BF16) for s in range(NSEQ)]
                hb = [Buf(ph, nc, "hb%d" % i, [128, 8, TT], F32) for i in range(3)]
                sq = Buf(ph, nc, "sq", [128, 8, TT], BF16)
                rstd = Buf(ph, nc, "rstd", [128, TT], F32)
                hn = [Buf(ph, nc, "hn%d" % i, [128, 8, TT], BF16) for i in range(2)]
                qx = Buf(ph, nc, "qx", [128, 8, TT], BF16)
                px = [Buf(ph, nc, "px%d" % i, [128, 2, TT], BF16) for i in range(2)]
                ao = [Buf(ph, nc, "ao%d" % i, [128, 8, TT], BF16) for i in range(2)]
                rden = [Buf(ph, nc, "rdx%d" % i, [128, TT], F32) for i in range(2)]
                with ExitStack() as p1:
                    Wkv = Buf(p1, nc, "Wkv", [128, 8, 2 * D], BF16)
                    load_w(p1, Wkv, xa_wkv[layer], 8, 2 * D, gain="memg", stage=stage)
                    load_w(ph, Wq, xa_wq[layer], 8, D, gain="xa%d" % layer, stage=stage)
                    load_w(ph, Wo, xa_wo[layer], 8, D, stage=stage)
                    P.dma("sp", hb[0].t[:], HB3[:, :, 0:TT], sb=hb[0].r, dram=RB[0])
                    nb = 0
                    for s in range(NSEQ):
                        for oc in range(8):
                            bk = banks[nb % 8]
                            nb += 1
                            mm_group(bk, (0, NMEM), [Wkv.t[:, k, oc * 128:(oc + 1) * 128] for k in range(8)], [mT[s].t[:, k, :] for k in range(8)], reads=Wkv.rc(oc * 128, (oc + 1) * 128) + [mT[s].r])
                            evac(kTm[s].t[:, oc, :], bk, writes=[kTm[s].r])
                        for mc in range(2):
                            for hf in range(2):
                                bk = banks[nb % 8]
                                nb += 1
                                mm_group(bk, (0, 512), [mT[s].t[:, k, mc * 128:(mc + 1) * 128] for k in range(8)],
                                         [Wkv.t[:, k, D + hf * 512:D + (hf + 1) * 512] for k in range(8)], reads=Wkv.rc(D + hf * 512, D + (hf + 1) * 512) + [mT[s].r])
                                evac(vm[s].t[:, mc, hf * 512:(hf + 1) * 512], bk, writes=[vm[s].r])
                xscale = 256.0 ** -0.5
                NTL = 2 * NT

                def wo_part(i, ocs):
                    hbi = hb[i % 3]
                    aoi = ao[i % 2]
                    for oc in ocs:
                        bk = banks[1 + oc % 2]
                        mm_group(bk, (0, TT), [Wo.t[:, k, oc * 128:(oc + 1) * 128] for k in range(8)], [aoi.t[:, k, :] for k in range(8)],
                                 reads=Wo.rc(oc * 128, (oc + 1) * 128) + [aoi.r])
                        P.op("dve", I(V.tensor_tensor, out=hbi.t[:, oc, :], in0=bk[0][:], in1=hbi.t[:, oc, :], op=ALU.add),
                             writes=[bk[1], hbi.r])
                    if ocs[-1] == 7:
                        P.dma("pool", HB3[:, :, i * TT:(i + 1) * TT], hbi.t[:], sb=hbi.r, dram=RB[i], load=False)

                norm_tile(hb[0], sq, rstd, hn[0], banks[0])
                for i in range(NTL):
                    b = i % 2
                    s = i // NT
                    if i + 1 < NTL:
                        P.dma("sp", hb[(i + 1) % 3].t[:], HB3[:, :, (i + 1) * TT:(i + 2) * TT], sb=hb[(i + 1) % 3].r, dram=RB[i + 1])
                    for oc in range(8):
                        bk = banks[1 + oc % 2]
                        mm_group(bk, (0, TT), [Wq.t[:, k, oc * 128:(oc + 1) * 128] for k in range(8)], [hn[b].t[:, k, :] for k in range(8)],
                                 reads=Wq.rc(oc * 128, (oc + 1) * 128) + [hn[b].r])
                        evac(qx.t[:, oc, :], bk, writes=[qx.r])
                    for h in range(4):
                        pb = px[h % 2]
                        rd = rden[h % 2]
                        for mc in range(2):
                            bk = banks[3 + mc]
                            mm_group(bk, (0, TT), [kTm[s].t[:, 2 * h + d, mc * 128:(mc + 1) * 128] for d in range(2)],
                                     [qx.t[:, 2 * h + d, :] for d in range(2)], reads=[kTm[s].r, qx.r])
                            P.op("act", I(A.activation, out=pb.t[:, mc, :], in_=bk[0][:], func=AF.Exp, scale=xscale),
                                 writes=[bk[1], pb.r])
                        if h == 0 and i + 1 < NTL:
                            norm_a(hb[(i + 1) % 3], sq)
                        if h == 2 and i + 1 < NTL:
                            norm_b(hb[(i + 1) % 3], sq, rstd, hn[1 - b], banks[0])
                        if i > 0:
                            wo_part(i - 1, [2 * h, 2 * h + 1])
                        bk = banks[5]
                        mm_group(bk, (0, TT), [ones.t[:], ones.t[:]], [pb.t[:, 0, :], pb.t[:, 1, :]], reads=[pb.r, ones.r])
                        P.op("act", I(A.activation, out=rd.t[:], in_=bk[0][:], func=AF.Ln), writes=[bk[1], rd.r])
                        P.op("act", I(A.activation, out=rd.t[:], in_=rd.t[:], func=AF.Exp, scale=-1.0), writes=[rd.r])
                        for d in range(2):
                            bk = banks[6 + d]
                            mm_group(bk, (0, TT), [vm[s].t[:, mc, (2 * h + d) * 128:(2 * h + d + 1) * 128] for mc in range(2)],
                                     [pb.t[:, mc, :] for mc in range(2)], reads=[pb.r, vm[s].r])
                            P.op("dve", I(V.tensor_tensor, out=ao[b].t[:, 2 * h + d, :], in0=bk[0][:], in1=rd.t[:], op=ALU.mult),
                                 reads=[rd.r], writes=[bk[1], ao[b].r])
                wo_part(NTL - 1, list(range(8)))
                P.barrier()
            dbg_dump()
            if not more():
                break
            with ExitStack() as ph:
                stage = None
                Wu = Buf(ph, nc, "Wu", [128, 8, 2 * DFF], BF16)
                Wd = Buf(ph, nc, "Wd", [128, 16, D], BF16)
                hb = [Buf(ph, nc, "hb%d" % i, [128, 8, TT], F32) for i in range(2)]
                hh = [Buf(ph, nc, "hh%d" % i, [128, 8, 2], F32) for i in range(2)]
                sq = Buf(ph, nc, "sq", [128, 8, TT], BF16)
                sqh = Buf(ph, nc, "sqh", [128, 8, 2], BF16)
                rstd = Buf(ph, nc, "rstd", [128, TT], F32)
                rsth = Buf(ph, nc, "rsth", [128, 2], F32)
                hn = [Buf(ph, nc, "hn%d" % i, [128, 8, TT], BF16) for i in range(2)]
                hnh = [Buf(ph, nc, "hnh%d" % i, [128, 8, 2], BF16) for i in range(2)]
                Gc = [Buf(ph, nc, "Gc%d" % i, [128, TT + 2], F32) for i in range(2)]
                Tc = [Buf(ph, nc, "Tc%d" % i, [128, TT], F32) for i in range(2)]
                Tg = [Buf(ph, nc, "Tg%d" % i, [128, TT], BF16) for i in range(2)]
                aT = [Buf(ph, nc, "aT%d" % i, [128, 16, TT], BF16) for i in range(1)]
                Us = [Buf(ph, nc, "Us%d" % i, [128, TT], BF16) for i in range(2)]
                load_w(ph, Wu, ffn_w_up[layer], 8, 2 * DFF, gain="ffn%d" % layer, stage=stage, order=[0, 4, 1, 5, 2, 6, 3, 7])
                load_w(ph, Wd, ffn_w_down[layer], 16, D, stage=stage)

                def load_tile(i, b):
                    s, ii = divmod(i, NT)
                    t0 = i * TT
                    P.dma("sp", hb[b].t[:], HB3[:, :, t0:t0 + TT], sb=hb[b].r, dram=RB[i])
                    P.op("pool", I(G.memset, hh[b].t[:], 0.0), writes=[hh[b].r])
                    if ii > 0:
                        P.dma("sp", hh[b].t[:, :, 0:1], HB3[:, :, t0 - 1:t0], sb=hh[b].r, dram=RB[i - 1], slow=True)
                    if ii < NT - 1:
                        P.dma("sp", hh[b].t[:, :, 1:2], HB3[:, :, t0 + TT:t0 + TT + 1], sb=hh[b].r, dram=RB[i + 1], slow=True)

                load_tile(0, 0)
                for i in range(2 * NT):
                    b = i % 2
                    t0 = i * TT
                    if i + 1 < 2 * NT:
                        load_tile(i + 1, 1 - b)
                    if i == 0:
                        norm_tile(hb[0], sq, rstd, hn[0], banks[0])
                        norm_tile(hh[0], sqh, rsth, hnh[0], banks[1], n=2)
                    for c in range(16):
                        if c == 10 and i + 1 < 2 * NT:
                            norm_a(hb[1 - b], sq)
                            norm_a(hh[1 - b], sqh, n=2)
                        bg, bu = banks[2 + c % 2], banks[4 + c % 2]
                        gc, tc, tg, us = Gc[c % 2], Tc[c % 2], Tg[c % 2], Us[c % 2]
                        mm_group(bg, (0, TT), [Wu.t[:, k, DFF + c * 128:DFF + (c + 1) * 128] for k in range(8)], [hn[b].t[:, k, :] for k in range(8)], reads=Wu.rc(DFF + c * 128, DFF + (c + 1) * 128) + [hn[b].r])
                        mm_group(banks[1], (8, 10), [Wu.t[:, k, DFF + c * 128:DFF + (c + 1) * 128] for k in range(8)], [hnh[b].t[:, k, :] for k in range(8)], reads=Wu.rc(DFF + c * 128, DFF + (c + 1) * 128) + [hnh[b].r])
                        mm_group(bu, (0, TT), [Wu.t[:, k, c * 128:(c + 1) * 128] for k in range(8)], [hn[b].t[:, k, :] for k in range(8)], reads=Wu.rc(c * 128, (c + 1) * 128) + [hn[b].r])
                        P.op("act", I(A.copy, out=gc.t[:, 1:TT + 1], in_=bg[0][:]), writes=[bg[1], gc.r])
                        P.op("act", I(A.copy, out=us.t[:], in_=bu[0][:]), writes=[bu[1], us.r])
                        P.op("dve", I(V.tensor_copy, out=gc.t[:, 0:TT + 2:TT + 1], in_=banks[1][0][:, 8:10]), writes=[banks[1][1], gc.r])
                        P.op("act", I(A.activation, out=tc.t[:], in_=gc.t[:, 1:TT + 1], func=AF.Identity, scale=vcol("fw%d_1" % layer, c), bias=vcol("fb%d" % layer, c)),
                             reads=[gc.r, vecs.r], writes=[tc.r])
                        P.op("dve", [I(V.scalar_tensor_tensor, out=tc.t[:], in0=gc.t[:, 0:TT], scalar=vcol("fw%d_0" % layer, c), in1=tc.t[:], op0=ALU.mult, op1=ALU.add),
                                     I(V.scalar_tensor_tensor, out=tc.t[:], in0=gc.t[:, 2:TT + 2], scalar=vcol("fw%d_2" % layer, c), in1=tc.t[:], op0=ALU.mult, op1=ALU.add)],
                             reads=[gc.r, vecs.r], writes=[tc.r])
                        P.op("act", I(A.activation, out=tg.t[:], in_=tc.t[:], func=AF.Gelu),
                             reads=[tc.r, vecs.r], writes=[tg.r])
                        P.op("dve", I(V.tensor_tensor, out=aT[0].t[:, c, :], in0=us.t[:], in1=tg.t[:], op=ALU.mult),
                             reads=[tg.r, us.r], writes=[aT[0].r])
                    if i + 1 < 2 * NT:
                        norm_b(hb[1 - b], sq, rstd, hn[1 - b], banks[0])
                        norm_b(hh[1 - b], sqh, rsth, hnh[1 - b], banks[0], n=2, col0=0)
                    for oc in range(8):
                        bk = banks[6 + oc % 2]
                        mm_group(bk, (0, TT), [Wd.t[:, k, oc * 128:(oc + 1) * 128] for k in range(16)], [aT[0].t[:, k, :] for k in range(16)], reads=Wd.rc(oc * 128, (oc + 1) * 128) + [aT[0].r])
                        P.op("dve", I(V.tensor_tensor, out=hb[b].t[:, oc, :], in0=bk[0][:], in1=hb[b].t[:, oc, :], op=ALU.add),
                             writes=[bk[1], hb[b].r])
                    P.dma("pool", HA3[:, :, t0:t0 + TT], hb[b].t[:], sb=hb[b].r, dram=RA[i], load=False)
                P.barrier()
            last_h = [HA3, RA]
            dbg_dump()

        final_norm(last_h[0], last_h[1], outT3, list(range(2 * NT)))
        nops, nwait = P.emit()
        print("program: ops=%d waits=%d" % (nops, nwait))
    return nc


def fmajor(v):
    return np.ascontiguousarray(v.reshape(-1, 128).T)


def make_vecs(inp):
    vecs = np.zeros((128, NV), np.float32)

    def put(name, v):
        a = fmajor(np.asarray(v, np.float32))
        vecs[:, VOFF[name]:VOFF[name] + a.shape[1]] = a

    put("memg", inp["mem_norm_g"])
    for l in range(DEPTH):
        put("mix%d" % l, inp["mix_norm_g"][l])
        put("xa%d" % l, inp["xa_norm_g"][l])
        put("ffn%d" % l, inp["ffn_norm_g"][l])
        for tap in range(3):
            put("fw%d_%d" % (l, tap), inp["ffn_conv_w"][l, tap])
        put("fb%d" % l, inp["ffn_conv_b"][l])
    put("fin", inp["final_norm_g"])
    for j in range(2):
        for tap in range(3):
            put("cc%d_%d" % (j, tap), inp["conv_c"][j, tap])
    return vecs


def make_nab(rpb):
    rpb = np.asarray(rpb, np.float32)
    kr = np.arange(2)[:, None, None, None]
    kc = np.arange(64)[None, :, None, None]
    mi = np.arange(NMI)[None, None, :, None]
    qc = np.arange(64)[None, None, None, :]
    dr = kr - (mi - 6)
    cs = np.clip(qc - 8, 0, 48)
    colv = (kc >= cs) & (kc <= cs + 15)
    dc = kc - qc
    ri = np.clip(dr + 7, 0, 14)
    ci = np.clip(dc + 15, 0, 30)
    ri, ci, colv, dr = np.broadcast_arrays(ri, ci, colv, dr)
    out = np.full((2, 4, 2, 2, 64, NMI, 64), NEG, np.float32)
    for ty in range(2):
        valid = colv & ((dr >= -4) & (dr <= 3) if ty == 0 else (dr >= -7) & (dr <= 7))
        g = rpb[:, :, ri, ci]
        out[:, :, ty] = np.where(valid[None, None], g, np.float32(NEG))
    return np.ascontiguousarray(out.reshape(2, 4, 2, 128, NMI * 64))


_CONST = {}


def consts():
    if not _CONST:
        t = np.arange(S, dtype=np.int64)
        ph = (np.outer(t, t) % S).astype(np.float64) * (2 * np.pi / S)
        _CONST["cosT"] = np.cos(ph).astype(ml_dtypes.bfloat16)
        _CONST["nsinT"] = (-np.sin(ph)).astype(ml_dtypes.bfloat16)
        c = np.arange(128, dtype=np.int64)
        pc = (np.outer(c, c) % 128).astype(np.float64) * (2 * np.pi / 128)
        sc = 1.0 / np.sqrt(float(S) * 128.0)
        _CONST["csc"] = np.concatenate([np.cos(pc) * sc, np.sin(pc) * sc], axis=1).astype(np.float32)
    return _CONST


_NC = {}


def kernel(**inp):
    ncores = 8
    inp = {k: np.asarray(v) for k, v in inp.items()}
    cst = consts()
    vecs = make_vecs(inp)
    nab = make_nab(inp["rpb"])
    shared = {
        "vecs": vecs, "nab": nab, "cosT": cst["cosT"], "nsinT": cst["nsinT"], "csc": cst["csc"],
    }
    for k in ["w_in_ab", "w_out_ab", "w_in_c", "w_out_c", "xa_wq", "xa_wkv", "xa_wo", "ffn_w_up", "ffn_w_down"]:
        shared[k] = np.ascontiguousarray(inp[k], dtype=np.float32)
    in_maps = []
    for c in range(ncores):
        xs = inp["x"][2 * c:2 * c + 2].reshape(NTOK, D)
        ms = inp["mem"][2 * c:2 * c + 2].reshape(NSEQ * NMEM, D)
        m = dict(shared)
        m["xT"] = np.ascontiguousarray(xs.T, dtype=np.float32)
        m["memT"] = np.ascontiguousarray(ms.T, dtype=np.float32)
        in_maps.append(m)
    if "nc" not in _NC:
        _NC["nc"] = build()
    res = run_bass_kernel_spmd(_NC["nc"], in_maps, core_ids=list(range(ncores)))
    out = np.empty((16, S, D), np.float32)
    for c in range(ncores):
        out[2 * c:2 * c + 2] = res.results[c]["outT"].T.reshape(2, S, D)
    return out
```
